# Optimizing a Trainium2 kernel written in Bass

```python
import jax, jax.numpy as jnp
from jax import lax
import numpy as np

D_MODEL = 1024
BATCH = 4
SEQ = 8192
DEPTH = 2

HEAD_DIM = 64
ATTN_WIDTH = D_MODEL // 2
N_HEADS_A = ATTN_WIDTH // HEAD_DIM
Q_BLOCK = 128
CONV_CH = D_MODEL // 2
CONV_K = 3
POOL_WINDOWS = (2, 4, 8, 16)
POOL_GROUPS = len(POOL_WINDOWS)
POOL_CG = D_MODEL // POOL_GROUPS
D_FF = 4 * D_MODEL
RMS_EPS = 1e-6
MIX_IN = 3 * ATTN_WIDTH + N_HEADS_A + 3 * CONV_CH
MIX_OUT = ATTN_WIDTH + CONV_CH

kernel_name = "fox_shortconv_pool_hybrid"


def rms_norm(x, g):
    xf = x.astype(jnp.float32)
    y = xf * lax.rsqrt(jnp.mean(xf * xf, axis=-1, keepdims=True) + RMS_EPS)
    return (y * g.astype(jnp.float32)).astype(x.dtype)


def forgetting_attention(q, k, v, f_logit):
    b, s, h, dh = q.shape
    nblk = s // Q_BLOCK
    log_f = jax.nn.log_sigmoid(f_logit.astype(jnp.float32))
    cum = jnp.cumsum(log_f, axis=1)
    cum_k = cum.transpose(0, 2, 1)
    q_blocks = q.reshape(b, nblk, Q_BLOCK, h, dh).transpose(1, 0, 2, 3, 4)
    cum_q_blocks = cum.reshape(b, nblk, Q_BLOCK, h).transpose(1, 0, 3, 2)
    q_pos = jnp.arange(s).reshape(nblk, Q_BLOCK)
    k_pos = jnp.arange(s)
    scale = dh ** -0.5
    neg = jnp.finfo(jnp.float32).min

    def one_block(args):
        qi, cqi, pi = args
        logits = jnp.einsum('bqhd,bkhd->bhqk', qi, k).astype(jnp.float32) * scale
        logits = logits + cqi[..., None] - cum_k[:, :, None, :]
        mask = k_pos[None, :] <= pi[:, None]
        logits = jnp.where(mask, logits, neg)
        p = jax.nn.softmax(logits, axis=-1)
        return jnp.einsum('bhqk,bkhd->bqhd', p.astype(v.dtype), v)

    out = lax.map(one_block, (q_blocks, cum_q_blocks, q_pos))
    return out.transpose(1, 0, 2, 3, 4).reshape(b, s, h * dh)


def causal_dwconv3(u, w):
    s = u.shape[1]
    up = jnp.pad(u, ((0, 0), (CONV_K - 1, 0), (0, 0)))
    return w[0] * up[:, 0:s] + w[1] * up[:, 1:s + 1] + w[2] * up[:, 2:s + 2]


def attn_conv_mixer(h, w_in, b_f, conv_w, w_out):
    b, s, _ = h.shape
    proj = h @ w_in
    a = ATTN_WIDTH
    splits = [a, 2 * a, 3 * a, 3 * a + N_HEADS_A,
              3 * a + N_HEADS_A + CONV_CH, 3 * a + N_HEADS_A + 2 * CONV_CH]
    q, k, v, f_logit, b_gate, c_gate, x_in = jnp.split(proj, splits, axis=-1)
    shp = (b, s, N_HEADS_A, HEAD_DIM)
    att = forgetting_attention(q.reshape(shp), k.reshape(shp), v.reshape(shp), f_logit + b_f)
    conv = b_gate * causal_dwconv3(c_gate * x_in, conv_w)
    return jnp.concatenate([att, conv], axis=-1) @ w_out


def causal_mean_pool_minus_self(u, window):
    s = u.shape[1]
    uf = u.astype(jnp.float32)
    cs = jnp.pad(jnp.cumsum(uf, axis=1), ((0, 0), (1, 0), (0, 0)))
    lagged = jnp.pad(cs, ((0, 0), (window - 1, 0), (0, 0)))[:, :s]
    count = jnp.minimum(jnp.arange(1, s + 1), window).astype(jnp.float32)[None, :, None]
    return ((cs[:, 1:] - lagged) / count - uf).astype(u.dtype)


def pool_mixer(h, pool_w, pool_scale):
    b, s, d = h.shape
    groups = jnp.split(h, POOL_GROUPS, axis=-1)
    pooled = jnp.stack([causal_mean_pool_minus_self(g, w) for g, w in zip(groups, POOL_WINDOWS)],
                       axis=2)
    y = jnp.einsum('bsgc,gcd->bsgd', pooled, pool_w).reshape(b, s, d)
    return y * pool_scale


def sq_relu_mlp(h, w_up, w_down):
    return jnp.square(jax.nn.relu(h @ w_up)) @ w_down


def setup_inputs(seed: int = 0) -> dict:
    key = jax.random.key(seed)
    ks = jax.random.split(key, 20)
    f32 = jnp.float32

    def nrm(k, shape, scale):
        return jax.random.normal(k, shape, f32) * scale

    def gain(k):
        return 1.0 + 0.05 * jax.random.normal(k, (D_MODEL,), f32)

    return {
        "x": jax.random.normal(ks[0], (BATCH, SEQ, D_MODEL), f32),
        "norm_mix_0": gain(ks[1]),
        "w_in_0": nrm(ks[2], (D_MODEL, MIX_IN), D_MODEL ** -0.5),
        "b_f_0": 2.0 + 0.5 * jax.random.normal(ks[3], (N_HEADS_A,), f32),
        "conv_w_0": nrm(ks[4], (CONV_K, CONV_CH), CONV_K ** -0.5),
        "w_out_0": nrm(ks[5], (MIX_OUT, D_MODEL), MIX_OUT ** -0.5),
        "norm_ffn_0": gain(ks[6]),
        "w_up_0": nrm(ks[7], (D_MODEL, D_FF), D_MODEL ** -0.5),
        "w_down_0": nrm(ks[8], (D_FF, D_MODEL), D_FF ** -0.5),
        "norm_mix_1": gain(ks[9]),
        "pool_w_1": nrm(ks[10], (POOL_GROUPS, POOL_CG, POOL_CG), POOL_CG ** -0.5),
        "pool_scale_1": 1.0 + 0.05 * jax.random.normal(ks[11], (D_MODEL,), f32),
        "norm_ffn_1": gain(ks[12]),
        "w_up_1": nrm(ks[13], (D_MODEL, D_FF), D_MODEL ** -0.5),
        "w_down_1": nrm(ks[14], (D_FF, D_MODEL), D_FF ** -0.5),
        "final_norm": gain(ks[15]),
    }


def reference(x, norm_mix_0, w_in_0, b_f_0, conv_w_0, w_out_0, norm_ffn_0, w_up_0, w_down_0,
              norm_mix_1, pool_w_1, pool_scale_1, norm_ffn_1, w_up_1, w_down_1, final_norm):
    mix_params = [(norm_mix_0, w_in_0, b_f_0, conv_w_0, w_out_0),
                  (norm_mix_1, pool_w_1, pool_scale_1)]
    ffn_params = [(norm_ffn_0, w_up_0, w_down_0), (norm_ffn_1, w_up_1, w_down_1)]
    h = x
    for i in range(DEPTH):
        mp = mix_params[i]
        if i % 2 == 0:
            h = h + attn_conv_mixer(rms_norm(h, mp[0]), *mp[1:])
        else:
            h = h + pool_mixer(rms_norm(h, mp[0]), *mp[1:])
        g, w_up, w_down = ffn_params[i]
        h = h + sq_relu_mlp(rms_norm(h, g), w_up, w_down)
    return rms_norm(h, final_norm)
```

```python
import numpy as np
import concourse.bass as bass
import concourse.mybir as mybir
from concourse.bass_utils import run_bass_kernel_spmd
from contextlib import ExitStack

F32 = mybir.dt.float32
BF16 = mybir.dt.bfloat16
AF = mybir.ActivationFunctionType
ALU = mybir.AluOpType

COMPUTE = ("pe", "act", "dve", "pool")
NCORES = 8
D = 1024
NB = 512
TOWN = 4096
TK = 8192
TQ = 4608
EPS = 1e-6
W_IN_COLS = (0, 512, 1024, 1544, 2056, 2568)
GQ, GK, GV, GBG, GCG, GXI = range(6)


class Op:
    __slots__ = ("eng", "fn", "deps", "is_dma", "dma_sem", "dma_val", "inc", "cnt", "epoch")


class Prog:
    def __init__(self, epoch_size=16000):
        self.ops = {e: [] for e in ("pe", "act", "dve", "pool", "sp")}
        self.last_w = {}
        self.readers = {}
        self.dma_last = {}
        self.epoch_size = epoch_size
        self.all_ops = []
        self.last_op = {}

    def _dep(self, op, prod, raw):
        if prod is None or prod is op:
            return
        if (not prod.is_dma) and (not op.is_dma) and prod.eng == op.eng:
            if (not raw) or op.eng == "pe":
                return
        op.deps.append(prod)

    def op(self, eng, fn, reads=(), writes=(), dma_key=None):
        o = Op()
        o.eng = eng
        o.fn = fn
        o.deps = []
        o.is_dma = dma_key is not None
        o.inc = False
        o.dma_sem = None
        for r in reads:
            self._dep(o, self.last_w.get(r), True)
        for w in writes:
            self._dep(o, self.last_w.get(w), False)
            for rd in self.readers.get(w, ()):
                self._dep(o, rd, False)
        res = []
        for d in o.deps:
            if d.is_dma:
                res.append(("dma", d.dma_sem, self.dma_last[d.dma_sem]))
            else:
                res.append(("eng", d))
        o.deps = res
        for r in reads:
            self.readers.setdefault(r, []).append(o)
        for w in writes:
            self.last_w[w] = o
            self.readers[w] = []
        if o.is_dma:
            o.dma_sem = dma_key
            v = self.dma_last.get(dma_key, 0) + 16
            self.dma_last[dma_key] = v
            o.dma_val = v
        else:
            self.last_op[eng] = o
        self.ops[eng].append(o)
        self.all_ops.append(o)
        return o

    def barrier(self):
        lasts = dict(self.last_op)
        dmas = dict(self.dma_last)
        for e in ("pe", "act", "dve", "pool", "sp"):
            o = Op()
            o.eng = e
            o.fn = None
            o.is_dma = False
            o.inc = False
            o.dma_sem = None
            o.deps = [("eng", p) for pe_, p in lasts.items() if pe_ != e]
            o.deps += [("dma", k, v) for k, v in dmas.items()]
            self.ops[e].append(o)
            self.all_ops.append(o)
        self.last_w = {}
        self.readers = {}

    def finalize(self):
        for o in self.all_ops:
            for d in o.deps:
                if d[0] == "eng":
                    d[1].inc = True
        self.n_epochs = {}
        for e in COMPUTE:
            c = 0
            ep = 0
            for o in self.ops[e]:
                if o.is_dma or o.fn is None:
                    continue
                if o.inc:
                    if c >= self.epoch_size:
                        ep += 1
                        c = 0
                    c += 1
                    o.cnt = c
                    o.epoch = ep
            self.n_epochs[e] = ep + 1

    def sem_names(self):
        names = []
        for e in COMPUTE:
            for ep in range(self.n_epochs[e]):
                names.append(("eng", e, ep))
        for k in self.dma_last:
            names.append(("dma", k))
        return names

    def run_engine(self, eng_name, eng, sems):
        waited = {}
        for o in self.ops[eng_name]:
            for d in o.deps:
                if d[0] == "dma":
                    key = ("dma", d[1])
                    val = d[2]
                else:
                    p = d[1]
                    key = ("eng", p.eng, p.epoch)
                    val = p.cnt
                if waited.get(key, 0) >= val:
                    continue
                waited[key] = val
                eng.wait_ge(sems[key], val)
            if o.fn is None:
                continue
            ins = o.fn(eng)
            if o.is_dma:
                ins.then_inc(sems[("dma", o.dma_sem)], 16)
            elif o.inc:
                ins.then_inc(sems[("eng", o.eng, o.epoch)], 1)

    def final_waits(self, eng, sems):
        for k, v in self.dma_last.items():
            eng.wait_ge(sems[("dma", k)], v)


class Arena:
    def __init__(self, ap, nwords):
        self.ap = ap
        self.n = nwords
        self.top = 0
        self.mark_ = 0

    def alloc(self, dtype, free_shape, parts=128):
        n = 1
        for s in free_shape:
            n *= s
        esz = 4 if dtype == F32 else 2
        words = (n * esz + 3) // 4
        words = (words + 7) // 8 * 8
        assert self.top + words <= self.n, ("SBUF arena overflow", self.top, words, self.n)
        a = self.ap[:, self.top:self.top + words]
        self.top += words
        if dtype != F32:
            a = a.bitcast(dtype)
        a = a[:, 0:n]
        if len(free_shape) == 2:
            a = a.rearrange("p (a b) -> p a b", b=free_shape[1])
        elif len(free_shape) == 3:
            a = a.rearrange("p (a b c) -> p a b c", b=free_shape[1], c=free_shape[2])
        return a

    def mark(self):
        self.mark_ = self.top

    def reset(self):
        self.top = self.mark_


def build_nc():
    nc = bass.Bass("TRN2", target_bir_lowering=False)

    def din(name, shape, dt=F32):
        return nc.dram_tensor(name, list(shape), dt, kind="ExternalInput").ap()

    def dscr(name, shape, dt=BF16):
        return nc.dram_tensor(name, list(shape), dt, kind="Internal").ap()

    xown = din("xown", [TOWN, D])
    xctx = din("xctx", [TOWN, D])
    flags_d = din("flags", [128, 4])
    invcnt_d = din("invcnt", [128, 64])
    vecs_d = din("vecs", [128, 6, 8])
    bf_d = din("bf32", [128, 32])
    convw_d = din("convw", [128, 4, 3])
    w_in = din("w_in", [D, 3080])
    w_out = din("w_out", [D, D])
    w_up = [din("w_up0", [D, 4096]), din("w_up1", [D, 4096])]
    w_dn = [din("w_dn0", [4096, D]), din("w_dn1", [4096, D])]
    pool_w = din("pool_w", [4, 256, 256])
    out_d = nc.dram_tensor("out", [TOWN, D], F32, kind="ExternalOutput").ap()

    wf_s = dscr("wf_s", [128, 8, 8])
    wo_s = dscr("wo_s", [128, 8, D])
    wup_s = [dscr("wup_s0", [8, 128, 8, 512]), dscr("wup_s1", [8, 128, 8, 512])]
    wdn_s = [dscr("wdn_s0", [8, 128, 32, 128]), dscr("wdn_s1", [8, 128, 32, 128])]
    poolw_s = dscr("poolw_s", [4, 128, 2, 256])
    kscr = dscr("kscr", [8, 64, TK])
    vscr = dscr("vscr", [8, 128, 64, 128])
    qscr = dscr("qscr", [8, 65, TQ])
    attscr = dscr("attscr", [8, 64, TQ])
    convscr = dscr("convscr", [4, 128, TQ])

    P = Prog()
    st = ExitStack()
    E = st.enter_context
    NW = 50000
    arena_t = E(nc.sbuf_tensor("arena", [128, NW], F32))
    A = Arena(arena_t, NW)
    banks = [E(nc.psum_tensor("bank%d" % i, [128, 512], F32)) for i in range(8)]

    iot = A.alloc(F32, [128])
    identF = A.alloc(F32, [128])
    triu = A.alloc(F32, [128])
    onesF = A.alloc(F32, [128])
    identB = A.alloc(BF16, [128])
    maskT = A.alloc(BF16, [128])
    onesB = A.alloc(BF16, [128])
    epsc = A.alloc(F32, [1])
    onec = A.alloc(F32, [1])
    flags = A.alloc(F32, [4])
    invcnt = A.alloc(F32, [4, 16])
    vecs = A.alloc(F32, [6, 8])
    bf32 = A.alloc(F32, [32])
    convw = A.alloc(F32, [4, 3])
    negD = A.alloc(F32, [64, 8])
    negDm = A.alloc(F32, [32, 8])
    prevsum = A.alloc(F32, [8])
    zhalo = A.alloc(F32, [8, 16])
    A.mark()

    P.op("pool", lambda e: e.iota(iot, [[1, 128]], base=0, channel_multiplier=-1,
                                  allow_small_or_imprecise_dtypes=True), writes=["iot"])
    P.op("dve", lambda e: e.tensor_scalar(identF, iot, 0.0, None, ALU.is_equal), reads=["iot"], writes=["identF"])
    P.op("dve", lambda e: e.tensor_scalar(identB, iot, 0.0, None, ALU.is_equal), reads=["iot"], writes=["identB"])
    P.op("dve", lambda e: e.tensor_scalar(triu, iot, 0.0, None, ALU.is_ge), reads=["iot"], writes=["triu"])
    P.op("dve", lambda e: e.tensor_scalar(maskT, iot, 0.0, -32768.0, ALU.is_lt, ALU.mult), reads=["iot"], writes=["maskT"])
    P.op("pool", lambda e: e.memset(onesF, 1.0), writes=["onesF"])
    P.op("pool", lambda e: e.memset(onesB, 1.0), writes=["onesB"])
    P.op("pool", lambda e: e.memset(epsc, EPS), writes=["epsc"])
    P.op("pool", lambda e: e.memset(onec, 1.0), writes=["onec"])
    P.op("pool", lambda e: e.memset(prevsum, 0.0), writes=["prevsum"])
    P.op("pool", lambda e: e.memset(zhalo, 0.0), writes=["zhalo"])
    for dst, src, k in ((flags, flags_d, "flags"), (invcnt, invcnt_d.rearrange("p (g j) -> p g j", j=16), "invcnt"),
                        (vecs, vecs_d, "vecs"), (bf32, bf_d, "bf32"), (convw, convw_d, "convw")):
        P.op("sp", (lambda d_, s_: (lambda e: e.dma_start(out=d_, in_=s_)))(dst, src), writes=[k], dma_key="ld_const")

    def cast(dst, src, key, sem):
        P.op("pool", lambda e: e.dma_start(out=dst, in_=src), writes=[key], dma_key=sem)

    cast(wf_s, w_in[:, 1536:1544].rearrange("(k p) c -> p k c", p=128), "wf", "cv_kv")
    late_casts = [(wo_s[:, kc, :], w_out[kc * 128:(kc + 1) * 128, :], "wo", "cv_o") for kc in range(8)]
    for l in range(2):
        for g in range(8):
            for kc in range(8):
                late_casts.append((wup_s[l][g, :, kc, :], w_up[l][kc * 128:(kc + 1) * 128, g * 512:(g + 1) * 512], ("wup", l), "cv_f%d" % l))
        for oc in range(8):
            for kq in range(4):
                late_casts.append((wdn_s[l][oc, :, kq * 8:(kq + 1) * 8, :],
                                   w_dn[l][kq * 1024:(kq + 1) * 1024, oc * 128:(oc + 1) * 128].rearrange("(k p) c -> p k c", p=128),
                                   ("wdn", l), "cv_f%d" % l))
        if l == 0:
            for g in range(4):
                late_casts.append((poolw_s[g], pool_w[g].rearrange("(cc p) d -> p cc d", p=128), "poolw", "cv_o"))

    rot = {}

    def nxt(name, n):
        v = rot.get(name, 0)
        rot[name] = v + 1
        return v % n

    def issue_x_load(src_rows, nts, xtok, xtok_key, ld_sem):
        P.op("sp", lambda e: e.dma_start(out=xtok[:, 0:nts, :], in_=src_rows.rearrange("(ts p) d -> p ts d", p=128)),
             writes=[xtok_key], dma_key=ld_sem)

    def transpose_x(nts, xtok, xtok_key, hT, hkey):
        N = nts * 128
        for kc in range(8):
            b = nxt("tb", 2)
            bk = banks[b]
            for ts in range(nts):
                P.op("pe", (lambda bk_, ts_, kc_: (lambda e: e.transpose(bk_[:, ts_ * 128:(ts_ + 1) * 128],
                                                                          xtok[:, ts_, kc_ * 128:(kc_ + 1) * 128], identF)))(bk, ts, kc),
                     reads=[xtok_key, "identF"], writes=[("bank", b)])
            P.op("dve", (lambda bk_, kc_: (lambda e: e.tensor_copy(hT[:, kc_, 0:N], bk_[:, 0:N])))(bk, kc),
                 reads=[("bank", b)], writes=[(hkey, kc)])

    def rmsnorm(hT, hkey, N, gi, sq, xn, rstd, t1, out_dtype_bf16=True, xnkey="xn"):
        for kc in range(8):
            if kc not in (1, 3, 5):
                P.op("act", (lambda kc_: (lambda e: e.activation(sq[:, kc_, 0:N], hT[:, kc_, 0:N], AF.Square)))(kc),
                     reads=[(hkey, kc)], writes=[("sq", kc)])
            else:
                P.op("pool", (lambda kc_: (lambda e: e.tensor_tensor(sq[:, kc_, 0:N], hT[:, kc_, 0:N], hT[:, kc_, 0:N], ALU.mult)))(kc),
                     reads=[(hkey, kc)], writes=[("sq", kc)])
        for kc in range(8):
            P.op("pe", (lambda kc_: (lambda e: e.matmul(banks[2][:, 0:N], onesB, sq[:, kc_, 0:N],
                                                       start=(kc_ == 0), stop=(kc_ == 7))))(kc),
                 reads=[("sq", kc), "onesB"], writes=[("bank", 2)])
        P.op("act", lambda e: e.activation(t1[:, 0:N], banks[2][:, 0:N], AF.Ln, bias=epsc, scale=1.0 / D),
             reads=[("bank", 2), "epsc"], writes=["t1"])
        P.op("act", lambda e: e.activation(rstd[:, 0:N], t1[:, 0:N], AF.Exp, scale=-0.5),
             reads=["t1"], writes=["rstd"])
        for kc in range(8):
            P.op("dve", (lambda kc_: (lambda e: e.scalar_tensor_tensor(xn[:, kc_, 0:N], hT[:, kc_, 0:N], vecs[:, gi, kc_:kc_ + 1],
                                                                        rstd[:, 0:N], ALU.mult, ALU.mult)))(kc),
                 reads=[(hkey, kc), "rstd", "vecs"], writes=[(xnkey, kc)])

    xtok = [A.alloc(F32, [4, D]) for _ in range(2)]
    hT = A.alloc(F32, [8, NB])
    sq = A.alloc(BF16, [8, NB])
    xn = A.alloc(BF16, [8, NB])
    rstd = A.alloc(F32, [NB])
    t1 = A.alloc(F32, [NB])
    rstd2 = A.alloc(F32, [NB])
    rstdT = A.alloc(F32, [4])
    wt = [A.alloc(BF16, [8, 512]) for _ in range(6)]
    wf = A.alloc(BF16, [8, 8])
    qst = A.alloc(BF16, [4, NB])
    kst = A.alloc(BF16, [4, NB])
    vaug = [A.alloc(BF16, [8, 128]) for _ in range(4)]
    bgs = A.alloc(BF16, [4, NB])
    cgs = A.alloc(F32, [4, NB])
    ubuf = [A.alloc(F32, [4, NB + 2]) for _ in range(2)]
    ctmp = [A.alloc(F32, [NB]) for _ in range(2)]
    convst = A.alloc(BF16, [4, NB])
    zf = A.alloc(F32, [32])
    ef = A.alloc(F32, [32])
    lnvs = [A.alloc(F32, [32]) for _ in range(2)]
    psb = [A.alloc(F32, [4, 8]) for _ in range(2)]
    deferred = []
    drow = A.alloc(BF16, [NB])

    for s in range(4):
        P.op("pool", (lambda s_: (lambda e: e.memset(vaug[s_], 1.0)))(s), writes=[("vaug", s)])
    for s in range(2):
        P.op("pool", (lambda s_: (lambda e: e.memset(ubuf[s_], 0.0)))(s), writes=[("u", s, c) for c in range(4)])
    issue_x_load(xctx[0:NB, :], 4, xtok[0], ("xtok", 0), "ld_x0")
    stg = A.alloc(F32, [8, 512])
    def stage_group(g):
        c0 = W_IN_COLS[g]
        P.op("sp", (lambda c0_: (lambda e: e.dma_start(out=stg, in_=w_in[:, c0_:c0_ + 512].rearrange("(k p) c -> p k c", p=128))))(c0),
             writes=["stgA", "stgB"], dma_key="ld_stg")
        P.op("act", (lambda g_: (lambda e: e.activation(wt[g_][:, 0:4, :], stg[:, 0:4, :], AF.Copy)))(g),
             reads=["stgA"], writes=[("wt", g, 0)])
        P.op("dve", (lambda g_: (lambda e: e.tensor_copy(wt[g_][:, 4:8, :], stg[:, 4:8, :])))(g),
             reads=["stgB"], writes=[("wt", g, 1)])

    stage_group(GK)
    stage_group(GV)
    later_groups = [GQ, GBG, GCG, GXI]
    P.op("sp", lambda e: e.dma_start(out=wf, in_=wf_s), reads=["wf"], writes=["wf_sb"], dma_key="ld_wf")

    def p1_block(src_rows, full, tile0, qtok0, xs, uslot, uprev_scale, next_src):
        xt_ = xtok[xs]
        for kc in range(8):
            b = (0, 1, 3, 4)[nxt("tb4", 4)]
            bk = banks[b]
            for ts in range(4):
                P.op("pe", (lambda bk_, ts_, kc_: (lambda e: e.transpose(bk_[:, ts_ * 128:(ts_ + 1) * 128],
                                                                          xt_[:, ts_, kc_ * 128:(kc_ + 1) * 128], identF)))(bk, ts, kc),
                     reads=[("xtok", xs), "identF"], writes=[("bank", b)])
            P.op("dve", (lambda bk_, kc_: (lambda e: e.tensor_scalar(xn[:, kc_, :], bk_[:, :], vecs[:, 0, kc_:kc_ + 1], None, ALU.mult)))(bk, kc),
                 reads=[("bank", b), "vecs"], writes=[("xn", kc), ("bankrd", b)])
            P.op("act", (lambda bk_, kc_: (lambda e: e.activation(sq[:, kc_, :], bk_[:, :], AF.Square)))(bk, kc),
                 reads=[("bank", b), ("bankrd", b)], writes=[("sq", kc)])
        if next_src is not None:
            issue_x_load(next_src, 4, xtok[1 - xs], ("xtok", 1 - xs), "ld_x%d" % (1 - xs))

        def stats():
            for kc in range(8):
                P.op("pe", (lambda kc_: (lambda e: e.matmul(banks[2][:, :], onesB, sq[:, kc_, :], start=(kc_ == 0), stop=(kc_ == 7))))(kc),
                     reads=[("sq", kc), "onesB"], writes=[("bank", 2)])
            P.op("act", lambda e: e.activation(t1, banks[2][:, :], AF.Ln, bias=epsc, scale=1.0 / D),
                 reads=[("bank", 2), "epsc"], writes=["t1"])
            P.op("act", lambda e: e.activation(rstd, t1, AF.Exp, scale=-0.5), reads=["t1"], writes=["rstd"])
            if full:
                P.op("dve", lambda e: e.tensor_tensor(rstd2, rstd, rstd, ALU.mult), reads=["rstd"], writes=["rstd2"])
            for ts in range(4):
                P.op("pe", (lambda ts_: (lambda e: e.transpose(banks[2][:, ts_ * 128:(ts_ + 1) * 128], rstd[:, ts_ * 128:(ts_ + 1) * 128], identF)))(ts),
                     reads=["rstd", "identF"], writes=[("bank", 2)])
            P.op("dve", lambda e: e.tensor_copy(rstdT, banks[2][:, :].rearrange("p (a b) -> p a b", b=128)[:, :, 0]),
                 reads=[("bank", 2)], writes=["rstdT"])

        stats_done = [False]
        nchunks = [0]
        kpend = []
        cpend = []
        groups = [GK, GV, GQ, GBG, GCG, GXI] if full else [GK, GV]
        for g in groups:
            ws = g
            wtile = wt[ws]
            if g == GV:
                for ts in range(4):
                    b = 5 + nxt("vb", 2)
                    for kc in range(8):
                        P.op("pe", (lambda b_, ts_, kc_, wtile_: (lambda e: e.matmul(banks[b_][:, :], xn[:, kc_, ts_ * 128:(ts_ + 1) * 128],
                                                                                      wtile_[:, kc_, :], start=(kc_ == 0), stop=(kc_ == 7))))(b, ts, kc, wtile),
                             reads=[("xn", kc), ("wt", ws, 0), ("wt", ws, 1)], writes=[("bank", b)])
                    vs = nxt("vaug", 4)
                    if not stats_done[0]:
                        stats()
                        stats_done[0] = True
                    P.op("dve", (lambda b_, vs_, ts_: (lambda e: e.tensor_scalar(vaug[vs_][:, :, 0:64],
                                                                                  banks[b_][:, :].rearrange("p (h c) -> p h c", c=64),
                                                                                  rstdT[:, ts_:ts_ + 1], None, ALU.mult)))(b, vs, ts),
                         reads=[("bank", b), "rstdT"], writes=[("vaug", vs)])
                    P.op("sp", (lambda vs_, tl_: (lambda e: e.dma_start(out=vscr[:, :, tl_, :].rearrange("h p c -> p h c"), in_=vaug[vs_])))(vs, tile0 + ts),
                         reads=[("vaug", vs)], writes=["vscr"], dma_key="st_v%d" % vs)
                continue
            for c in range(4):
                b = (3, 4, 0, 1)[nxt("fb", 4)]
                for kc in range(8):
                    P.op("pe", (lambda b_, c_, kc_, wtile_: (lambda e: e.matmul(banks[b_][:, :], wtile_[:, kc_, c_ * 128:(c_ + 1) * 128],
                                                                                 xn[:, kc_, :], start=(kc_ == 0), stop=(kc_ == 7))))(b, c, kc, wtile),
                         reads=[("xn", kc), ("wt", ws, 0), ("wt", ws, 1)], writes=[("bank", b)])
                nchunks[0] += 1
                if nchunks[0] == 2 and not stats_done[0]:
                    stats()
                    stats_done[0] = True
                if g == GQ:
                    P.op("dve", (lambda b_, c_: (lambda e: e.tensor_tensor(qst[:, c_, :], banks[b_][:, :], rstd, ALU.mult)))(b, c),
                         reads=[("bank", b), "rstd"], writes=[("qst", c)])
                elif g == GK:
                    def kev(b=b, c=c):
                        P.op("dve", (lambda b_, c_: (lambda e: e.tensor_tensor(kst[:, c_, :], banks[b_][:, :], rstd, ALU.mult)))(b, c),
                             reads=[("bank", b), "rstd"], writes=[("kst", c)])
                    if not stats_done[0]:
                        kpend.append(kev)
                    else:
                        while kpend:
                            kpend.pop(0)()
                        kev()
                elif g == GBG:
                    P.op("dve", (lambda b_, c_: (lambda e: e.tensor_tensor(bgs[:, c_, :], banks[b_][:, :], rstd, ALU.mult)))(b, c),
                         reads=[("bank", b), "rstd"], writes=[("bgs", c)])
                elif g == GCG:
                    P.op("dve", (lambda b_, c_: (lambda e: e.tensor_tensor(cgs[:, c_, :], banks[b_][:, :], rstd2, ALU.mult)))(b, c),
                         reads=[("bank", b), "rstd2"], writes=[("cgs", c)])
                elif g == GXI:
                    u = ubuf[uslot]
                    up = ubuf[1 - uslot]
                    if uprev_scale is None:
                        P.op("pool", (lambda c_, u_, up_: (lambda e: e.tensor_copy(u_[:, c_, 0:2], up_[:, c_, NB:NB + 2])))(c, u, up),
                             reads=[("u", 1 - uslot, c)], writes=[("u", uslot, c)])
                    else:
                        P.op("pool", (lambda c_, u_, up_: (lambda e: e.tensor_scalar(u_[:, c_, 0:2], up_[:, c_, NB:NB + 2],
                                                                                      flags[:, 1:2], None, ALU.mult)))(c, u, up),
                             reads=[("u", 1 - uslot, c), "flags"], writes=[("u", uslot, c)])
                    P.op("dve", (lambda b_, c_, u_: (lambda e: e.tensor_tensor(u_[:, c_, 2:NB + 2], banks[b_][:, :], cgs[:, c_, :], ALU.mult)))(b, c, u),
                         reads=[("bank", b), ("cgs", c), ("u", uslot, c)], writes=[("u", uslot, c)])
                    def convops(c=c, u=u):
                        ct = nxt("ctmp", 2)
                        tt = ctmp[ct]
                        P.op("dve", (lambda c_, u_, tt_: (lambda e: e.tensor_scalar(tt_, u_[:, c_, 2:NB + 2], convw[:, c_, 2:3], None, ALU.mult)))(c, u, tt),
                             reads=[("u", uslot, c), "convw"], writes=[("ctmp", ct)])
                        P.op("dve", (lambda c_, u_, tt_: (lambda e: e.scalar_tensor_tensor(tt_, u_[:, c_, 1:NB + 1], convw[:, c_, 1:2], tt_, ALU.mult, ALU.add)))(c, u, tt),
                             reads=[("u", uslot, c), ("ctmp", ct)], writes=[("ctmp", ct)])
                        P.op("dve", (lambda c_, u_, tt_: (lambda e: e.scalar_tensor_tensor(tt_, u_[:, c_, 0:NB], convw[:, c_, 0:1], tt_, ALU.mult, ALU.add)))(c, u, tt),
                             reads=[("u", uslot, c), ("ctmp", ct)], writes=[("ctmp", ct)])
                        P.op("pool", (lambda c_, tt_: (lambda e: e.tensor_tensor(convst[:, c_, :], tt_, bgs[:, c_, :], ALU.mult)))(c, tt),
                             reads=[("ctmp", ct), ("bgs", c)], writes=[("convst", c)])
                    cpend.append(convops)
            while cpend:
                cpend.pop(0)()
            if g == GQ:
                for half in range(2):
                    P.op("sp", (lambda hf: (lambda e: e.dma_start(
                        out=qscr[:, 0:64, qtok0:qtok0 + NB].rearrange("(c two) d t -> two d c t", two=2)[hf],
                        in_=qst[hf * 64:(hf + 1) * 64, :, :])))(half),
                        reads=[("qst", c) for c in range(4)], writes=["qscr"], dma_key="st_q")
            elif g == GK:
                while deferred:
                    deferred.pop(0)()
                for half in range(2):
                    P.op("sp", (lambda hf: (lambda e: e.dma_start(
                        out=kscr[:, :, tile0 * 128:tile0 * 128 + NB].rearrange("(c two) d t -> two d c t", two=2)[hf],
                        in_=kst[hf * 64:(hf + 1) * 64, :, :])))(half),
                        reads=[("kst", c) for c in range(4)], writes=["kscr"], dma_key="st_k")
            elif g == GXI:
                P.op("sp", lambda e: e.dma_start(out=convscr[:, :, qtok0:qtok0 + NB].rearrange("c p t -> p c t"), in_=convst),
                     reads=[("convst", c) for c in range(4)], writes=["convscr"], dma_key="st_c")
        for ts in range(4):
            for kc in range(8):
                P.op("pe", (lambda ts_, kc_: (lambda e: e.matmul(banks[7][:, ts_ * 8:(ts_ + 1) * 8], xn[:, kc_, ts_ * 128:(ts_ + 1) * 128],
                                                                 wf[:, kc_, :], start=(kc_ == 0), stop=(kc_ == 7))))(ts, kc),
                     reads=[("xn", kc), "wf_sb"], writes=[("bank", 7)])
        lv = lnvs[nxt("lnv", 2)]
        lvk = ("lnv", id(lv))
        for ts in range(4):
            P.op("dve", (lambda ts_: (lambda e: e.scalar_tensor_tensor(zf[:, ts_ * 8:(ts_ + 1) * 8], banks[7][:, ts_ * 8:(ts_ + 1) * 8],
                                                                        rstdT[:, ts_:ts_ + 1], bf32[:, ts_ * 8:(ts_ + 1) * 8], ALU.mult, ALU.add)))(ts),
                 reads=[("bank", 7), "bf32", "rstdT"], writes=["zf"])
        P.op("act", lambda e: e.activation(ef, zf, AF.Exp, scale=-1.0), reads=["zf"], writes=["ef"])
        P.op("act", lambda e: e.activation(lv, ef, AF.Ln, bias=onec, scale=1.0), reads=["ef", "onec"], writes=[lvk])

        pk = nxt("psb", 2)
        psb_ = psb[pk]
        P.op("pool", lambda e: e.tensor_copy(psb_[:, 0, :], prevsum), reads=["prevsum"], writes=[("psb", pk)])
        for ts in range(1, 4):
            P.op("pool", (lambda ts_: (lambda e: e.tensor_tensor(psb_[:, ts_, :], psb_[:, ts_ - 1, :], lv[:, (ts_ - 1) * 8:ts_ * 8], ALU.add)))(ts),
                 reads=[("psb", pk), lvk], writes=[("psb", pk)])
        P.op("pool", lambda e: e.tensor_tensor(prevsum, psb_[:, 3, :], lv[:, 24:32], ALU.add),
             reads=[("psb", pk), lvk], writes=["prevsum"])

        def dchain():
            for ts in range(4):
                tl = tile0 + ts
                reg = banks[7][:, 32 + ts * 8:32 + (ts + 1) * 8]
                P.op("pe", (lambda reg_, ts_: (lambda e: e.matmul(reg_, triu, lv[:, ts_ * 8:(ts_ + 1) * 8], start=True, stop=False)))(reg, ts),
                     reads=["triu", lvk], writes=[("bank", 7)])
                P.op("pe", (lambda reg_, ts_: (lambda e: e.matmul(reg_, onesF, psb_[:, ts_, :], start=False, stop=True)))(reg, ts),
                     reads=["onesF", ("psb", pk)], writes=[("bank", 7)])
                P.op("dve", (lambda reg_, tl_: (lambda e: e.tensor_copy(negD[:, tl_, :], reg_)))(reg, tl),
                     reads=[("bank", 7)], writes=[("negD", tl)])
                if tl < 32:
                    P.op("dve", (lambda reg_, tl_: (lambda e: e.tensor_scalar(negDm[:, tl_, :], reg_, flags[:, 0:1], None, ALU.add)))(reg, tl),
                         reads=[("bank", 7), "flags"], writes=[("negDm", tl)])
            if full:
                for ts in range(4):
                    tl = tile0 + ts
                    P.op("pe", (lambda ts_, tl_: (lambda e: e.transpose(banks[2][0:8, ts_ * 128:(ts_ + 1) * 128], negD[:, tl_, :], identF)))(ts, tl),
                         reads=[("negD", tl), "identF"], writes=[("bank", 2)])
                P.op("dve", lambda e: e.tensor_scalar(drow[0:8, :], banks[2][0:8, :], -8.0, None, ALU.mult),
                     reads=[("bank", 2)], writes=["drow"])
                P.op("sp", lambda e: e.dma_start(out=qscr[:, 64, qtok0:qtok0 + NB], in_=drow[0:8, :]),
                     reads=["drow"], writes=["qscr"], dma_key="st_q")

        deferred.append(dchain)

    srcs = [xctx[cb * NB:(cb + 1) * NB, :] for cb in range(8)] + [xown[ob * NB:(ob + 1) * NB, :] for ob in range(8)]
    blk = 0
    for cb in range(8):
        p1_block(srcs[blk], cb == 7, cb * 4, 0, blk % 2, 0, None, srcs[blk + 1])
        blk += 1
        if later_groups:
            stage_group(later_groups.pop(0))
    for ob in range(8):
        p1_block(srcs[blk], True, 32 + ob * 4, NB + ob * NB, blk % 2, (ob + 1) % 2,
                 True if ob == 0 else None, srcs[blk + 1] if blk + 1 < 16 else None)
        blk += 1
    while deferred:
        deferred.pop(0)()

    P.barrier()
    A.reset()

    KT = [A.alloc(BF16, [TK]) for _ in range(2)]
    VA = [A.alloc(BF16, [64, 128]) for _ in range(2)]
    QT = [A.alloc(BF16, [TQ]) for _ in range(2)]
    PT = [A.alloc(BF16, [NB]) for _ in range(4)]
    rec = [A.alloc(F32, [NB]) for _ in range(2)]
    atts = [A.alloc(BF16, [NB]) for _ in range(2)]
    for s in range(2):
        P.op("pool", (lambda s_: (lambda e: e.memset(KT[s_][64:65, :], 1.0)))(s), writes=[("KT", s)])
    while late_casts:
        cast(*late_casts.pop(0))

    qtiles = [(384, 128, 3968, False)] + [(NB + NB * i, NB, TOWN + NB * i, True) for i in range(8)]
    LOOK = 2

    def issue_head_loads(h):
        hs = h % 2
        P.op("sp", (lambda h_, hs_: (lambda e: e.dma_start(out=KT[hs_][0:64, :], in_=kscr[h_])))(h, hs),
             reads=["kscr"], writes=[("KT", hs)], dma_key="ld_kt%d" % hs)
        P.op("sp", (lambda h_, hs_: (lambda e: e.dma_start(out=VA[hs_], in_=vscr[h_])))(h, hs),
             reads=["vscr"], writes=[("VA", hs)], dma_key="ld_va%d" % hs)
        P.op("sp", (lambda h_, hs_: (lambda e: e.dma_start(out=QT[hs_][0:65, :], in_=qscr[h_])))(h, hs),
             reads=["qscr"], writes=[("QT", hs)], dma_key="ld_qt%d" % hs)

    steps = []
    for h in range(8):
        for qi, (qc0, N, kt0, masked) in enumerate(qtiles):
            nt = (kt0 + N) // 128
            if N == NB:
                order = [0] + list(range(nt - 4, nt)) + list(range(1, nt - 4))
            else:
                order = list(range(nt))
            for pos, j in enumerate(order):
                steps.append((h, qi, j, nt, pos == 0, pos == nt - 1))
    info = {}

    def emit_front(idx):
        h, qi, j, nt, first, last = steps[idx]
        hs = h % 2
        qc0, N, kt0, masked = qtiles[qi]
        if first:
            ab = 4 + nxt("accb", 3)
            info[(h, qi)] = ab
        s0 = j * 128
        c0 = max(0, s0 - kt0)
        diag = s0 >= kt0
        sb = nxt("sbank", 4)
        S = banks[sb]
        if not diag:
            P.op("pe", (lambda S_, hs_, s0_, c0_, qc0_, N_: (lambda e: e.matmul(
                S_[:, c0_:N_], KT[hs_][0:65, s0_:s0_ + 128], QT[hs_][0:65, qc0_ + c0_:qc0_ + N_],
                start=True, stop=True)))(S, hs, s0, c0, qc0, N),
                reads=[("KT", hs), ("QT", hs)], writes=[("bank", sb)])
        else:
            P.op("pe", (lambda S_, hs_, s0_, c0_, qc0_: (lambda e: e.matmul(
                S_[:, c0_:c0_ + 128], KT[hs_][0:65, s0_:s0_ + 128], QT[hs_][0:65, qc0_ + c0_:qc0_ + c0_ + 128],
                start=True, stop=False)))(S, hs, s0, c0, qc0),
                reads=[("KT", hs), ("QT", hs)], writes=[("bank", sb)])
            P.op("pe", (lambda S_, c0_: (lambda e: e.matmul(S_[:, c0_:c0_ + 128], identB, maskT, start=False, stop=True)))(S, c0),
                 reads=["identB", "maskT"], writes=[("bank", sb)])
            if c0 + 128 < N:
                P.op("pe", (lambda S_, hs_, s0_, c0_, qc0_, N_: (lambda e: e.matmul(
                    S_[:, c0_ + 128:N_], KT[hs_][0:65, s0_:s0_ + 128], QT[hs_][0:65, qc0_ + c0_ + 128:qc0_ + N_],
                    start=True, stop=True)))(S, hs, s0, c0, qc0, N),
                    reads=[("KT", hs), ("QT", hs)], writes=[("bank", sb)])
        bias = (negDm if (masked and j < 32) else negD)[:, j, h:h + 1]
        ps = nxt("pt", 4)
        P.op("act", (lambda S_, ps_, c0_, N_, bias_: (lambda e: e.activation(PT[ps_][:, c0_:N_], S_[:, c0_:N_], AF.Exp,
                                                                               bias=bias_, scale=0.125)))(S, ps, c0, N, bias),
             reads=[("bank", sb)], writes=[("PT", ps)])
        info[idx] = (ps, c0)

    def emit_back(idx):
        h, qi, j, nt, first, last = steps[idx]
        hs = h % 2
        qc0, N, kt0, masked = qtiles[qi]
        ps, c0 = info.pop(idx)
        ab = info[(h, qi)]
        acc = banks[ab]
        P.op("pe", (lambda acc_, hs_, j_, ps_, c0_, N_, first_, last_: (lambda e: e.matmul(
            acc_[:, c0_:N_], VA[hs_][:, j_, :], PT[ps_][:, c0_:N_], start=first_, stop=last_,
            skip_group_check=True)))(acc, hs, j, ps, c0, N, first, last),
            reads=[("VA", hs), ("PT", ps)], writes=[("bank", ab)])
        if last:
            rs = nxt("rec", 2)
            P.op("dve", (lambda acc_, rs_, N_: (lambda e: e.reciprocal(rec[rs_][0:64, 0:N_], acc_[64:128, 0:N_])))(acc, rs, N),
                 reads=[("bank", ab)], writes=[("rec", rs)])
            P.op("dve", (lambda acc_, rs_, N_: (lambda e: e.tensor_tensor(atts[rs_][0:64, 0:N_], acc_[0:64, 0:N_], rec[rs_][0:64, 0:N_], ALU.mult)))(acc, rs, N),
                 reads=[("bank", ab), ("rec", rs)], writes=[("atts", rs)])
            P.op("sp", (lambda h_, rs_, qc0_, N_: (lambda e: e.dma_start(out=attscr[h_, :, qc0_:qc0_ + N_], in_=atts[rs_][0:64, 0:N_])))(h, rs, qc0, N),
                 reads=[("atts", rs)], writes=["attscr"], dma_key="st_a%d" % rs)

    issue_head_loads(0)
    issue_head_loads(1)
    for idx in range(len(steps) + LOOK):
        if idx < len(steps):
            emit_front(idx)
        if idx >= LOOK:
            bi = idx - LOOK
            emit_back(bi)
            hprev = steps[bi][0]
            if (bi + 1 == len(steps) or steps[bi + 1][0] != hprev) and hprev + 2 < 8:
                issue_head_loads(hprev + 2)

    P.barrier()
    A.reset()

    xtk = [A.alloc(F32, [4, D]) for _ in range(2)]
    h3 = A.alloc(F32, [8, NB])
    sq3 = A.alloc(BF16, [8, NB])
    xn3 = A.alloc(BF16, [8, NB])
    rstd3 = A.alloc(F32, [NB])
    t13 = A.alloc(F32, [NB])
    hid = A.alloc(BF16, [32, NB])
    R1 = hid
    wo = R1[:, 0:16, :].rearrange("p a b -> p (a b)").rearrange("p (k c) -> p k c", c=D)
    attT = R1[:, 16:20, :]
    convT = A.alloc(BF16, [4, NB])
    wup = [A.alloc(BF16, [8, 512]) for _ in range(2)]
    wdn = [A.alloc(BF16, [32, 128]) for _ in range(2)]
    poolw = A.alloc(BF16, [4, 2, 256])
    zz = A.alloc(F32, [8, NB + 16])
    pa = A.alloc(F32, [NB + 16])
    pb = A.alloc(F32, [NB + 16])
    pc = A.alloc(F32, [NB + 16])
    pd = A.alloc(F32, [NB + 16])
    pooled = A.alloc(BF16, [8, NB])
    rr = [A.alloc(F32, [NB]) for _ in range(3)]
    outst = [A.alloc(F32, [D]) for _ in range(2)]

    for g in range(4):
        P.op("sp", (lambda g_: (lambda e: e.dma_start(out=poolw[:, g_, :, :], in_=poolw_s[g_])))(g),
             reads=["poolw"], writes=["poolw_sb"], dma_key="ld_pw")

    def ffn(l, N, gi):
        for kc in range(8):
            if kc % 2 == 0:
                P.op("act", (lambda kc_: (lambda e: e.activation(xn3[:, kc_, 0:N], h3[:, kc_, 0:N], AF.Copy,
                                                                 scale=vecs[:, gi, kc_:kc_ + 1])))(kc),
                     reads=[("h3", kc), "vecs"], writes=[("xn3", kc)])
            else:
                P.op("dve", (lambda kc_: (lambda e: e.tensor_scalar(xn3[:, kc_, 0:N], h3[:, kc_, 0:N], vecs[:, gi, kc_:kc_ + 1],
                                                                    None, ALU.mult)))(kc),
                     reads=[("h3", kc), "vecs"], writes=[("xn3", kc)])
        for kc in range(8):
            if kc % 2 == 1:
                P.op("act", (lambda kc_: (lambda e: e.activation(sq3[:, kc_, 0:N], h3[:, kc_, 0:N], AF.Square)))(kc),
                     reads=[("h3", kc)], writes=[("sq", kc)])
            else:
                P.op("pool", (lambda kc_: (lambda e: e.tensor_tensor(sq3[:, kc_, 0:N], h3[:, kc_, 0:N], h3[:, kc_, 0:N], ALU.mult)))(kc),
                     reads=[("h3", kc)], writes=[("sq", kc)])

        def stats():
            for kc in range(8):
                P.op("pe", (lambda kc_: (lambda e: e.matmul(banks[2][:, 0:N], onesB, sq3[:, kc_, 0:N],
                                                           start=(kc_ == 0), stop=(kc_ == 7))))(kc),
                     reads=[("sq", kc), "onesB"], writes=[("bank", 2)])
            P.op("act", lambda e: e.activation(t13[:, 0:N], banks[2][:, 0:N], AF.Ln, bias=epsc, scale=1.0 / D),
                 reads=[("bank", 2), "epsc"], writes=["t1"])
            P.op("act", lambda e: e.activation(rstd3[:, 0:N], t13[:, 0:N], AF.Exp, scale=-0.5),
                 reads=["t1"], writes=["rstd"])

        pending = []

        def evac(item):
            b, fc = item
            rs = nxt("rr", 3)
            P.op("dve", (lambda b_, rs_: (lambda e: e.scalar_tensor_tensor(rr[rs_][:, 0:N], banks[b_][:, 0:N], 0.0, rstd3[:, 0:N],
                                                                            ALU.max, ALU.mult)))(b, rs),
                 reads=[("bank", b), "rstd"], writes=[("rr", rs)])
            P.op("pool", (lambda fc_, rs_: (lambda e: e.tensor_tensor(hid[:, fc_, 0:N], rr[rs_][:, 0:N], rr[rs_][:, 0:N], ALU.mult)))(fc, rs),
                 reads=[("rr", rs)], writes=[("hid", fc)])

        nchunk = 0
        for g in range(8):
            ws = nxt("wup", 2)
            P.op("sp", (lambda g_, ws_: (lambda e: e.dma_start(out=wup[ws_], in_=wup_s[l][g_])))(g, ws),
                 reads=[("wup", l)], writes=[("wup_sb", ws)], dma_key="ld_wu%d" % ws)
            for c in range(4):
                fc = g * 4 + c
                b = 3 + nxt("fb3", 3)
                for kc in range(8):
                    P.op("pe", (lambda b_, c_, kc_, ws_: (lambda e: e.matmul(banks[b_][:, 0:N], wup[ws_][:, kc_, c_ * 128:(c_ + 1) * 128],
                                                                              xn3[:, kc_, 0:N], start=(kc_ == 0), stop=(kc_ == 7))))(b, c, kc, ws),
                         reads=[("xn3", kc), ("wup_sb", ws)], writes=[("bank", b)])
                nchunk += 1
                pending.append((b, fc))
                if nchunk == 2:
                    stats()
                if nchunk >= 2:
                    while pending:
                        evac(pending.pop(0))
        for oc in range(8):
            ws = nxt("wdn", 2)
            P.op("sp", (lambda oc_, ws_: (lambda e: e.dma_start(out=wdn[ws_], in_=wdn_s[l][oc_])))(oc, ws),
                 reads=[("wdn", l)], writes=[("wdn_sb", ws)], dma_key="ld_wd%d" % ws)
            b = 3 + nxt("fb3", 3)
            for kc in range(32):
                P.op("pe", (lambda b_, kc_, ws_: (lambda e: e.matmul(banks[b_][:, 0:N], wdn[ws_][:, kc_, :], hid[:, kc_, 0:N],
                                                                      start=(kc_ == 0), stop=(kc_ == 31))))(b, kc, ws),
                     reads=[("hid", kc), ("wdn_sb", ws)], writes=[("bank", b)])
            P.op("dve", (lambda b_, oc_: (lambda e: e.tensor_tensor(h3[:, oc_, 0:N], banks[b_][:, 0:N], h3[:, oc_, 0:N], ALU.add)))(b, oc),
                 reads=[("bank", b), ("h3", oc)], writes=[("h3", oc)])

    HID_ALL = [("hid", fc) for fc in range(32)]

    def issue_opnd_loads(qtok0, N):
        P.op("sp", lambda e: e.dma_start(out=wo, in_=wo_s), reads=["wo"], writes=HID_ALL, dma_key="ld_wo")
        for hf in range(2):
            P.op("sp", (lambda hf_: (lambda e: e.dma_start(
                out=attT[hf_ * 64:(hf_ + 1) * 64, :, 0:N],
                in_=attscr[:, :, qtok0:qtok0 + N].rearrange("(c two) d t -> two d c t", two=2)[hf_])))(hf),
                reads=["attscr"], writes=HID_ALL, dma_key="ld_wo")
        P.op("sp", lambda e: e.dma_start(out=convT[:, :, 0:N], in_=convscr[:, :, qtok0:qtok0 + N].rearrange("c p t -> p c t")),
             reads=["convscr"], writes=["convT"], dma_key="ld_wo")

    def p3_block(xs, nts, qtok0, out_rows, first_own, nxt_src, nxt_nts, nxt_q):
        N = nts * 128
        transpose_x(nts, xtk[xs], ("xtk", xs), h3, "h3")
        if nxt_src is not None:
            issue_x_load(nxt_src, nxt_nts, xtk[1 - xs], ("xtk", 1 - xs), "ld_x3%d" % (1 - xs))
        for oc in range(8):
            b = 3 + nxt("fb3", 3)
            for k in range(8):
                rhs_ = attT[:, k, 0:N] if k < 4 else convT[:, k - 4, 0:N]
                P.op("pe", (lambda b_, oc_, k_, rhs__: (lambda e: e.matmul(banks[b_][:, 0:N], wo[:, k_, oc_ * 128:(oc_ + 1) * 128],
                                                                            rhs__, start=(k_ == 0), stop=(k_ == 7))))(b, oc, k, rhs_),
                     reads=[HID_ALL[0], "convT"], writes=[("bank", b)])
            P.op("dve", (lambda b_, oc_: (lambda e: e.tensor_tensor(h3[:, oc_, 0:N], banks[b_][:, 0:N], h3[:, oc_, 0:N], ALU.add)))(b, oc),
                 reads=[("bank", b), ("h3", oc)], writes=[("h3", oc)])
        P.op("pe", lambda e: e.matmul(banks[2][0:8, 0:8], onesB[:, 0:8], onesB[:, 0:8], start=True, stop=True),
             reads=HID_ALL + ["onesB"], writes=[("bank", 2)])
        ffn(0, N, 1)
        if qtok0 < NB and nxt_q is not None:
            issue_opnd_loads(nxt_q, nxt_nts * 128)
        rmsnorm(h3, "h3", N, 2, sq3, zz[:, :, 16:16 + NB], rstd3, t13, xnkey="zz")
        if qtok0 < NB:
            for kc in range(8):
                P.op("dve", (lambda kc_: (lambda e: e.tensor_scalar(zhalo[:, kc_, 1:16], zz[:, kc_, 16 + N - 15:16 + N], flags[:, 1:2], None, ALU.mult)))(kc),
                     reads=[("zz", kc), "flags"], writes=[("zhalo", kc)])
            return
        for kc in range(8):
            g = kc // 2
            w = 2 << g
            eng = "pool" if kc in (0, 1, 2, 4) else "dve"
            P.op("act", (lambda kc_: (lambda e: e.activation(zz[:, kc_, 1:16], zhalo[:, kc_, 1:16], AF.Copy)))(kc),
                 reads=[("zhalo", kc)], writes=[("zzh", kc)])
            src = zz[:, kc, :]
            lo = 1
            sh = 1
            bufs = [pa, pb] if eng == "pool" else [pc, pd]
            bi = 0
            srckeys = [("zz", kc), ("zzh", kc)]
            while sh < w:
                dst = bufs[bi]
                lo2 = lo + sh
                P.op(eng, (lambda dst_, src_, lo2_, sh_: (lambda e: e.tensor_tensor(dst_[:, lo2_:16 + N], src_[:, lo2_:16 + N],
                                                                                     src_[:, lo2_ - sh_:16 + N - sh_], ALU.add)))(dst, src, lo2, sh),
                     reads=srckeys, writes=[("pbuf", eng, bi)])
                src = dst
                srckeys = [("pbuf", eng, bi)]
                lo = lo2
                sh *= 2
                bi = 1 - bi
            P.op("dve", (lambda kc_, src_, w_: (lambda e: e.scalar_tensor_tensor(pooled[:, kc_, 0:N], src_[:, 16:16 + N], 1.0 / w_,
                                                                                  zz[:, kc_, 16:16 + N], ALU.mult, ALU.subtract)))(kc, src, w),
                 reads=srckeys + [("zz", kc)], writes=[("pooled", kc)])
            if first_own:
                other = bufs[bi]
                P.op("dve", (lambda src_, other_, g_: (lambda e: e.tensor_tensor(other_[:, 0:16], src_[:, 16:32], invcnt[:, g_, :], ALU.mult)))(src, other, g),
                     reads=srckeys + ["invcnt"], writes=[("pbuf", eng, bi)])
                P.op("dve", (lambda kc_, other_: (lambda e: e.tensor_tensor(pooled[:, kc_, 0:16], other_[:, 0:16], zz[:, kc_, 16:32], ALU.subtract)))(kc, other),
                     reads=[("pbuf", eng, bi), ("zz", kc)], writes=[("pooled", kc)])
        for kc in range(8):
            P.op("pool", (lambda kc_: (lambda e: e.tensor_copy(zhalo[:, kc_, 1:16], zz[:, kc_, 16 + N - 15:16 + N])))(kc),
                 reads=[("zz", kc), ("zzh", kc)], writes=[("zhalo", kc)])
        for g in range(4):
            for dc in range(2):
                oc = 2 * g + dc
                b = 3 + nxt("fb3", 3)
                for cc in range(2):
                    P.op("pe", (lambda b_, g_, cc_, dc_: (lambda e: e.matmul(banks[b_][:, 0:N], poolw[:, g_, cc_, dc_ * 128:(dc_ + 1) * 128],
                                                                              pooled[:, 2 * g_ + cc_, 0:N], start=(cc_ == 0), stop=(cc_ == 1))))(b, g, cc, dc),
                         reads=[("pooled", 2 * g + cc), "poolw_sb"], writes=[("bank", b)])
                P.op("dve", (lambda b_, oc_: (lambda e: e.scalar_tensor_tensor(h3[:, oc_, 0:N], banks[b_][:, 0:N], vecs[:, 5, oc_:oc_ + 1],
                                                                                h3[:, oc_, 0:N], ALU.mult, ALU.add)))(b, oc),
                     reads=[("bank", b), ("h3", oc), "vecs"], writes=[("h3", oc)])
        ffn(1, N, 3)
        if nxt_q is not None:
            issue_opnd_loads(nxt_q, nxt_nts * 128)
        rmsnorm(h3, "h3", N, 4, sq3, zz[:, :, 16:16 + NB], rstd3, t13, xnkey="zz")
        for ts in range(nts):
            os_ = nxt("outst", 2)
            for half in range(2):
                b = (6, 7)[half] if (ts % 2 == 0) else (0, 1)[half]
                for k4 in range(4):
                    kc = half * 4 + k4
                    P.op("pe", (lambda b_, k4_, kc_, ts_: (lambda e: e.transpose(banks[b_][:, k4_ * 128:(k4_ + 1) * 128],
                                                                                  zz[:, kc_, 16 + ts_ * 128:16 + (ts_ + 1) * 128], identF)))(b, k4, kc, ts),
                         reads=[("zz", kc), "identF"], writes=[("bank", b)])
                P.op("act", (lambda b_, half_, os__: (lambda e: e.activation(outst[os__][:, half_ * 512:(half_ + 1) * 512], banks[b_][:, :], AF.Copy)))(b, half, os_),
                     reads=[("bank", b)], writes=[("outst", os_, half)])
            P.op("act", (lambda ts_, os__: (lambda e: e.dma_start(out=out_rows[ts_ * 128:(ts_ + 1) * 128, :], in_=outst[os__])))(ts, os_),
                 reads=[("outst", os_, 0), ("outst", os_, 1)], dma_key="st_o%d" % os_)

    issue_x_load(xctx[TOWN - 128:TOWN, :], 1, xtk[0], ("xtk", 0), "ld_x30")
    issue_opnd_loads(384, 128)
    p3_block(0, 1, 384, None, False, xown[0:NB, :], 4, NB)
    for ob in range(8):
        p3_block((ob + 1) % 2, 4, NB + ob * NB, out_d[ob * NB:(ob + 1) * NB, :], ob == 0,
                 xown[(ob + 1) * NB:(ob + 2) * NB, :] if ob < 7 else None, 4, NB + (ob + 1) * NB if ob < 7 else None)

    P.finalize()
    names = P.sem_names()
    sems = {k: E(nc.semaphore("s%d" % i)) for i, k in enumerate(names)}
    block = E(nc.Block())

    @block.tensor
    def _(e):
        P.run_engine("pe", e, sems)

    @block.scalar
    def _(e):
        P.run_engine("act", e, sems)

    @block.vector
    def _(e):
        P.run_engine("dve", e, sems)

    @block.gpsimd
    def _(e):
        P.run_engine("pool", e, sems)

    @block.sync
    def _(e):
        P.run_engine("sp", e, sems)
        P.final_waits(e, sems)

    st.close()
    return nc


_NC = None


def kernel(x, norm_mix_0, w_in_0, b_f_0, conv_w_0, w_out_0, norm_ffn_0, w_up_0, w_down_0,
           norm_mix_1, pool_w_1, pool_scale_1, norm_ffn_1, w_up_1, w_down_1, final_norm):
    global _NC
    f = lambda a: np.ascontiguousarray(np.asarray(a, dtype=np.float32))
    x = f(x)
    if _NC is None:
        _NC = build_nc()
    nc = _NC

    def pk(v):
        return f(v).reshape(8, 128).T

    vecs = np.ascontiguousarray(np.stack([pk(norm_mix_0), pk(norm_ffn_0), pk(norm_mix_1), pk(norm_ffn_1),
                                          pk(final_norm), pk(pool_scale_1)], axis=1))
    bf32 = np.ascontiguousarray(np.broadcast_to(np.tile(f(b_f_0), 4)[None, :], (128, 32)))
    convw = np.ascontiguousarray(f(conv_w_0).reshape(3, 4, 128).transpose(2, 1, 0))
    shared = {
        "vecs": vecs, "bf32": bf32, "convw": convw,
        "w_in": f(w_in_0), "w_out": f(w_out_0), "w_up0": f(w_up_0), "w_up1": f(w_up_1),
        "w_dn0": f(w_down_0), "w_dn1": f(w_down_1), "pool_w": f(pool_w_1),
    }
    in_maps = []
    for c in range(NCORES):
        b, half = c // 2, c % 2
        flags = np.zeros((128, 4), np.float32)
        flags[:, 0] = 0.0 if half == 1 else -30000.0
        flags[:, 1] = 1.0 if half == 1 else 0.0
        inv = np.zeros((4, 16), np.float32)
        for g in range(4):
            w = 2 << g
            for j in range(16):
                inv[g, j] = 1.0 / w if half == 1 else 1.0 / min(j + 1, w)
        invcnt = np.ascontiguousarray(np.broadcast_to(inv.reshape(1, 64), (128, 64)))
        m = dict(shared)
        m["xown"] = np.ascontiguousarray(x[b, half * TOWN:(half + 1) * TOWN])
        m["xctx"] = np.ascontiguousarray(x[b, 0:TOWN])
        m["flags"] = flags
        m["invcnt"] = invcnt
        in_maps.append(m)
    res = run_bass_kernel_spmd(nc, in_maps, core_ids=list(range(NCORES)))
    out = np.empty((4, 8192, D), np.float32)
    for c in range(NCORES):
        b, half = c // 2, c % 2
        out[b, half * TOWN:(half + 1) * TOWN] = res.results[c]["out"]
    return out
```

```python
import numpy as np
import concourse.bass as bass
import concourse.mybir as mybir
from concourse.bass_utils import run_bass_kernel_spmd
from contextlib import ExitStack

F32 = mybir.dt.float32
BF16 = mybir.dt.bfloat16
AF = mybir.ActivationFunctionType
ALU = mybir.AluOpType

COMPUTE = ("pe", "act", "dve", "pool")
NCORES = 8
D = 1024
NB = 512
TOWN = 4096
TK = 8192
TQ = 4608
EPS = 1e-6
W_IN_COLS = (0, 512, 1024, 1544, 2056, 2568)
GQ, GK, GV, GBG, GCG, GXI = range(6)


class Op:
    __slots__ = ("eng", "fn", "deps", "is_dma", "dma_sem", "dma_val", "inc", "cnt", "epoch")


class Prog:
    def __init__(self, epoch_size=16000):
        self.ops = {e: [] for e in ("pe", "act", "dve", "pool", "sp")}
        self.last_w = {}
        self.readers = {}
        self.dma_last = {}
        self.epoch_size = epoch_size
        self.all_ops = []
        self.last_op = {}

    def _dep(self, op, prod, raw):
        if prod is None or prod is op:
            return
        if (not prod.is_dma) and (not op.is_dma) and prod.eng == op.eng:
            if (not raw) or op.eng == "pe":
                return
        op.deps.append(prod)

    def op(self, eng, fn, reads=(), writes=(), dma_key=None):
        o = Op()
        o.eng = eng
        o.fn = fn
        o.deps = []
        o.is_dma = dma_key is not None
        o.inc = False
        o.dma_sem = None
        for r in reads:
            self._dep(o, self.last_w.get(r), True)
        for w in writes:
            self._dep(o, self.last_w.get(w), False)
            for rd in self.readers.get(w, ()):
                self._dep(o, rd, False)
        res = []
        for d in o.deps:
            if d.is_dma:
                res.append(("dma", d.dma_sem, self.dma_last[d.dma_sem]))
            else:
                res.append(("eng", d))
        o.deps = res
        for r in reads:
            self.readers.setdefault(r, []).append(o)
        for w in writes:
            self.last_w[w] = o
            self.readers[w] = []
        if o.is_dma:
            o.dma_sem = dma_key
            v = self.dma_last.get(dma_key, 0) + 16
            self.dma_last[dma_key] = v
            o.dma_val = v
        else:
            self.last_op[eng] = o
        self.ops[eng].append(o)
        self.all_ops.append(o)
        return o

    def barrier(self):
        lasts = dict(self.last_op)
        dmas = dict(self.dma_last)
        for e in ("pe", "act", "dve", "pool", "sp"):
            o = Op()
            o.eng = e
            o.fn = None
            o.is_dma = False
            o.inc = False
            o.dma_sem = None
            o.deps = [("eng", p) for pe_, p in lasts.items() if pe_ != e]
            o.deps += [("dma", k, v) for k, v in dmas.items()]
            self.ops[e].append(o)
            self.all_ops.append(o)
        self.last_w = {}
        self.readers = {}

    def finalize(self):
        for o in self.all_ops:
            for d in o.deps:
                if d[0] == "eng":
                    d[1].inc = True
        self.n_epochs = {}
        for e in COMPUTE:
            c = 0
            ep = 0
            for o in self.ops[e]:
                if o.is_dma or o.fn is None:
                    continue
                if o.inc:
                    if c >= self.epoch_size:
                        ep += 1
                        c = 0
                    c += 1
                    o.cnt = c
                    o.epoch = ep
            self.n_epochs[e] = ep + 1

    def sem_names(self):
        names = []
        for e in COMPUTE:
            for ep in range(self.n_epochs[e]):
                names.append(("eng", e, ep))
        for k in self.dma_last:
            names.append(("dma", k))
        return names

    def run_engine(self, eng_name, eng, sems):
        waited = {}
        for o in self.ops[eng_name]:
            for d in o.deps:
                if d[0] == "dma":
                    key = ("dma", d[1])
                    val = d[2]
                else:
                    p = d[1]
                    key = ("eng", p.eng, p.epoch)
                    val = p.cnt
                if waited.get(key, 0) >= val:
                    continue
                waited[key] = val
                eng.wait_ge(sems[key], val)
            if o.fn is None:
                continue
            ins = o.fn(eng)
            if o.is_dma:
                ins.then_inc(sems[("dma", o.dma_sem)], 16)
            elif o.inc:
                ins.then_inc(sems[("eng", o.eng, o.epoch)], 1)

    def final_waits(self, eng, sems):
        for k, v in self.dma_last.items():
            eng.wait_ge(sems[("dma", k)], v)


class Arena:
    def __init__(self, ap, nwords):
        self.ap = ap
        self.n = nwords
        self.top = 0
        self.mark_ = 0

    def alloc(self, dtype, free_shape, parts=128):
        n = 1
        for s in free_shape:
            n *= s
        esz = 4 if dtype == F32 else 2
        words = (n * esz + 3) // 4
        words = (words + 7) // 8 * 8
        assert self.top + words <= self.n, ("SBUF arena overflow", self.top, words, self.n)
        a = self.ap[:, self.top:self.top + words]
        self.top += words
        if dtype != F32:
            a = a.bitcast(dtype)
        a = a[:, 0:n]
        if len(free_shape) == 2:
            a = a.rearrange("p (a b) -> p a b", b=free_shape[1])
        elif len(free_shape) == 3:
            a = a.rearrange("p (a b c) -> p a b c", b=free_shape[1], c=free_shape[2])
        return a

    def mark(self):
        self.mark_ = self.top

    def reset(self):
        self.top = self.mark_


def build_nc():
    nc = bass.Bass("TRN2", target_bir_lowering=False)

    def din(name, shape, dt=F32):
        return nc.dram_tensor(name, list(shape), dt, kind="ExternalInput").ap()

    def dscr(name, shape, dt=BF16):
        return nc.dram_tensor(name, list(shape), dt, kind="Internal").ap()

    xown = din("xown", [TOWN, D])
    xctx = din("xctx", [TOWN, D])
    flags_d = din("flags", [128, 4])
    invcnt_d = din("invcnt", [128, 64])
    vecs_d = din("vecs", [128, 6, 8])
    bf_d = din("bf32", [128, 32])
    convw_d = din("convw", [128, 4, 3])
    w_in = din("w_in", [D, 3080])
    w_out = din("w_out", [D, D])
    w_up = [din("w_up0", [D, 4096]), din("w_up1", [D, 4096])]
    w_dn = [din("w_dn0", [4096, D]), din("w_dn1", [4096, D])]
    pool_w = din("pool_w", [4, 256, 256])
    out_d = nc.dram_tensor("out", [TOWN, D], F32, kind="ExternalOutput").ap()

    wf_s = dscr("wf_s", [128, 8, 8])
    wo_s = dscr("wo_s", [128, 8, D])
    wup_s = [dscr("wup_s0", [8, 128, 8, 512]), dscr("wup_s1", [8, 128, 8, 512])]
    wdn_s = [dscr("wdn_s0", [8, 128, 32, 128]), dscr("wdn_s1", [8, 128, 32, 128])]
    poolw_s = dscr("poolw_s", [4, 128, 2, 256])
    kscr = dscr("kscr", [8, 64, TK])
    vscr = dscr("vscr", [8, 128, 64, 128])
    qscr = dscr("qscr", [8, 65, TQ])
    attscr = dscr("attscr", [8, 64, TQ])
    convscr = dscr("convscr", [4, 128, TQ])

    P = Prog()
    st = ExitStack()
    E = st.enter_context
    NW = 52000
    arena_t = E(nc.sbuf_tensor("arena", [128, NW], F32))
    A = Arena(arena_t, NW)
    banks = [E(nc.psum_tensor("bank%d" % i, [128, 512], F32)) for i in range(8)]

    iot = A.alloc(F32, [128])
    identF = A.alloc(F32, [128])
    triu = A.alloc(F32, [128])
    onesF = A.alloc(F32, [128])
    identB = A.alloc(BF16, [128])
    maskT = A.alloc(BF16, [128])
    onesB = A.alloc(BF16, [128])
    epsc = A.alloc(F32, [1])
    onec = A.alloc(F32, [1])
    flags = A.alloc(F32, [4])
    invcnt = A.alloc(F32, [4, 16])
    vecs = A.alloc(F32, [6, 8])
    bf32 = A.alloc(F32, [32])
    convw = A.alloc(F32, [4, 3])
    negD = A.alloc(F32, [64, 8])
    negDm = A.alloc(F32, [32, 8])
    prevsum = A.alloc(F32, [8])
    zhalo = A.alloc(F32, [8, 16])
    A.mark()

    P.op("pool", lambda e: e.iota(iot, [[1, 128]], base=0, channel_multiplier=-1,
                                  allow_small_or_imprecise_dtypes=True), writes=["iot"])
    P.op("dve", lambda e: e.tensor_scalar(identF, iot, 0.0, None, ALU.is_equal), reads=["iot"], writes=["identF"])
    P.op("dve", lambda e: e.tensor_scalar(identB, iot, 0.0, None, ALU.is_equal), reads=["iot"], writes=["identB"])
    P.op("dve", lambda e: e.tensor_scalar(triu, iot, 0.0, None, ALU.is_ge), reads=["iot"], writes=["triu"])
    P.op("dve", lambda e: e.tensor_scalar(maskT, iot, 0.0, -32768.0, ALU.is_lt, ALU.mult), reads=["iot"], writes=["maskT"])
    P.op("pool", lambda e: e.memset(onesF, 1.0), writes=["onesF"])
    P.op("pool", lambda e: e.memset(onesB, 1.0), writes=["onesB"])
    P.op("pool", lambda e: e.memset(epsc, EPS), writes=["epsc"])
    P.op("pool", lambda e: e.memset(onec, 1.0), writes=["onec"])
    P.op("pool", lambda e: e.memset(prevsum, 0.0), writes=["prevsum"])
    P.op("pool", lambda e: e.memset(zhalo, 0.0), writes=["zhalo"])
    for dst, src, k in ((flags, flags_d, "flags"), (invcnt, invcnt_d.rearrange("p (g j) -> p g j", j=16), "invcnt"),
                        (vecs, vecs_d, "vecs"), (bf32, bf_d, "bf32"), (convw, convw_d, "convw")):
        P.op("sp", (lambda d_, s_: (lambda e: e.dma_start(out=d_, in_=s_)))(dst, src), writes=[k], dma_key="ld_const")

    def cast(dst, src, key, sem):
        P.op("pool", lambda e: e.dma_start(out=dst, in_=src), writes=[key], dma_key=sem)

    cast(wf_s, w_in[:, 1536:1544].rearrange("(k p) c -> p k c", p=128), "wf", "cv_kv")
    late_casts = [(wo_s[:, kc, :], w_out[kc * 128:(kc + 1) * 128, :], "wo", "cv_o") for kc in range(8)]
    for l in range(2):
        for g in range(8):
            for kc in range(8):
                late_casts.append((wup_s[l][g, :, kc, :], w_up[l][kc * 128:(kc + 1) * 128, g * 512:(g + 1) * 512], ("wup", l), "cv_f%d" % l))
        for oc in range(8):
            for kq in range(4):
                late_casts.append((wdn_s[l][oc, :, kq * 8:(kq + 1) * 8, :],
                                   w_dn[l][kq * 1024:(kq + 1) * 1024, oc * 128:(oc + 1) * 128].rearrange("(k p) c -> p k c", p=128),
                                   ("wdn", l), "cv_f%d" % l))
        if l == 0:
            for g in range(4):
                late_casts.append((poolw_s[g], pool_w[g].rearrange("(cc p) d -> p cc d", p=128), "poolw", "cv_o"))

    rot = {}

    def nxt(name, n):
        v = rot.get(name, 0)
        rot[name] = v + 1
        return v % n

    def issue_x_load(src_rows, nts, xtok, xtok_key, ld_sem):
        P.op("sp", lambda e: e.dma_start(out=xtok[:, 0:nts, :], in_=src_rows.rearrange("(ts p) d -> p ts d", p=128)),
             writes=[xtok_key], dma_key=ld_sem)

    def transpose_x(nts, xtok, xtok_key, hT, hkey):
        N = nts * 128
        for kc in range(8):
            b = nxt("tb", 2)
            bk = banks[b]
            for ts in range(nts):
                P.op("pe", (lambda bk_, ts_, kc_: (lambda e: e.transpose(bk_[:, ts_ * 128:(ts_ + 1) * 128],
                                                                          xtok[:, ts_, kc_ * 128:(kc_ + 1) * 128], identF)))(bk, ts, kc),
                     reads=[xtok_key, "identF"], writes=[("bank", b)])
            P.op("dve", (lambda bk_, kc_: (lambda e: e.tensor_copy(hT[:, kc_, 0:N], bk_[:, 0:N])))(bk, kc),
                 reads=[("bank", b)], writes=[(hkey, kc)])

    def rmsnorm(hT, hkey, N, gi, sq, xn, rstd, t1, out_dtype_bf16=True, xnkey="xn"):
        for kc in range(8):
            if kc not in (1, 3, 5):
                P.op("act", (lambda kc_: (lambda e: e.activation(sq[:, kc_, 0:N], hT[:, kc_, 0:N], AF.Square)))(kc),
                     reads=[(hkey, kc)], writes=[("sq", kc)])
            else:
                P.op("pool", (lambda kc_: (lambda e: e.tensor_tensor(sq[:, kc_, 0:N], hT[:, kc_, 0:N], hT[:, kc_, 0:N], ALU.mult)))(kc),
                     reads=[(hkey, kc)], writes=[("sq", kc)])
        for kc in range(8):
            P.op("pe", (lambda kc_: (lambda e: e.matmul(banks[2][:, 0:N], onesB, sq[:, kc_, 0:N],
                                                       start=(kc_ == 0), stop=(kc_ == 7))))(kc),
                 reads=[("sq", kc), "onesB"], writes=[("bank", 2)])
        P.op("act", lambda e: e.activation(t1[:, 0:N], banks[2][:, 0:N], AF.Ln, bias=epsc, scale=1.0 / D),
             reads=[("bank", 2), "epsc"], writes=["t1"])
        P.op("act", lambda e: e.activation(rstd[:, 0:N], t1[:, 0:N], AF.Exp, scale=-0.5),
             reads=["t1"], writes=["rstd"])
        for kc in range(8):
            P.op("dve", (lambda kc_: (lambda e: e.scalar_tensor_tensor(xn[:, kc_, 0:N], hT[:, kc_, 0:N], vecs[:, gi, kc_:kc_ + 1],
                                                                        rstd[:, 0:N], ALU.mult, ALU.mult)))(kc),
                 reads=[(hkey, kc), "rstd", "vecs"], writes=[(xnkey, kc)])

    xtok = [A.alloc(F32, [4, D]) for _ in range(2)]
    hT = A.alloc(F32, [8, NB])
    sq = A.alloc(BF16, [8, NB])
    xn = A.alloc(BF16, [8, NB])
    rstd = A.alloc(F32, [NB])
    t1 = A.alloc(F32, [NB])
    rstd2 = A.alloc(F32, [NB])
    rstdT = A.alloc(F32, [4])
    wt = [A.alloc(BF16, [8, 512]) for _ in range(6)]
    wf = A.alloc(BF16, [8, 8])
    qst = A.alloc(BF16, [4, NB])
    kst = A.alloc(BF16, [4, NB])
    vaug = [A.alloc(BF16, [8, 128]) for _ in range(4)]
    bgs = A.alloc(BF16, [4, NB])
    cgs = A.alloc(F32, [4, NB])
    ubuf = [A.alloc(F32, [4, NB + 2]) for _ in range(2)]
    ctmp = [A.alloc(F32, [NB]) for _ in range(2)]
    convst = A.alloc(BF16, [4, NB])
    zf = A.alloc(F32, [32])
    ef = A.alloc(F32, [32])
    lnvs = [A.alloc(F32, [32]) for _ in range(2)]
    psb = [A.alloc(F32, [4, 8]) for _ in range(2)]
    deferred = []
    drow = A.alloc(BF16, [NB])

    for s in range(4):
        P.op("pool", (lambda s_: (lambda e: e.memset(vaug[s_], 1.0)))(s), writes=[("vaug", s)])
    for s in range(2):
        P.op("pool", (lambda s_: (lambda e: e.memset(ubuf[s_], 0.0)))(s), writes=[("u", s, c) for c in range(4)])
    issue_x_load(xctx[0:NB, :], 4, xtok[0], ("xtok", 0), "ld_x0")
    stg = A.alloc(F32, [8, 512])
    def stage_group(g):
        c0 = W_IN_COLS[g]
        P.op("sp", (lambda c0_: (lambda e: e.dma_start(out=stg, in_=w_in[:, c0_:c0_ + 512].rearrange("(k p) c -> p k c", p=128))))(c0),
             writes=["stgA", "stgB"], dma_key="ld_stg")
        P.op("act", (lambda g_: (lambda e: e.activation(wt[g_][:, 0:4, :], stg[:, 0:4, :], AF.Copy)))(g),
             reads=["stgA"], writes=[("wt", g, 0)])
        P.op("dve", (lambda g_: (lambda e: e.tensor_copy(wt[g_][:, 4:8, :], stg[:, 4:8, :])))(g),
             reads=["stgB"], writes=[("wt", g, 1)])

    stage_group(GK)
    stage_group(GV)
    later_groups = [GQ, GBG, GCG, GXI]
    P.op("sp", lambda e: e.dma_start(out=wf, in_=wf_s), reads=["wf"], writes=["wf_sb"], dma_key="ld_wf")

    def p1_block(src_rows, full, tile0, qtok0, xs, uslot, uprev_scale, next_src):
        xt_ = xtok[xs]
        for kc in range(8):
            b = (0, 1, 3, 4)[nxt("tb4", 4)]
            bk = banks[b]
            for ts in range(4):
                P.op("pe", (lambda bk_, ts_, kc_: (lambda e: e.transpose(bk_[:, ts_ * 128:(ts_ + 1) * 128],
                                                                          xt_[:, ts_, kc_ * 128:(kc_ + 1) * 128], identF)))(bk, ts, kc),
                     reads=[("xtok", xs), "identF"], writes=[("bank", b)])
            P.op("dve", (lambda bk_, kc_: (lambda e: e.tensor_scalar(xn[:, kc_, :], bk_[:, :], vecs[:, 0, kc_:kc_ + 1], None, ALU.mult)))(bk, kc),
                 reads=[("bank", b), "vecs"], writes=[("xn", kc), ("bankrd", b)])
            P.op("act", (lambda bk_, kc_: (lambda e: e.activation(sq[:, kc_, :], bk_[:, :], AF.Square)))(bk, kc),
                 reads=[("bank", b), ("bankrd", b)], writes=[("sq", kc)])
        if next_src is not None:
            issue_x_load(next_src, 4, xtok[1 - xs], ("xtok", 1 - xs), "ld_x%d" % (1 - xs))

        def stats():
            for kc in range(8):
                P.op("pe", (lambda kc_: (lambda e: e.matmul(banks[2][:, :], onesB, sq[:, kc_, :], start=(kc_ == 0), stop=(kc_ == 7))))(kc),
                     reads=[("sq", kc), "onesB"], writes=[("bank", 2)])
            P.op("act", lambda e: e.activation(t1, banks[2][:, :], AF.Ln, bias=epsc, scale=1.0 / D),
                 reads=[("bank", 2), "epsc"], writes=["t1"])
            P.op("act", lambda e: e.activation(rstd, t1, AF.Exp, scale=-0.5), reads=["t1"], writes=["rstd"])
            if full:
                P.op("dve", lambda e: e.tensor_tensor(rstd2, rstd, rstd, ALU.mult), reads=["rstd"], writes=["rstd2"])
            for ts in range(4):
                P.op("pe", (lambda ts_: (lambda e: e.transpose(banks[2][:, ts_ * 128:(ts_ + 1) * 128], rstd[:, ts_ * 128:(ts_ + 1) * 128], identF)))(ts),
                     reads=["rstd", "identF"], writes=[("bank", 2)])
            P.op("dve", lambda e: e.tensor_copy(rstdT, banks[2][:, :].rearrange("p (a b) -> p a b", b=128)[:, :, 0]),
                 reads=[("bank", 2)], writes=["rstdT"])

        stats_done = [False]
        nchunks = [0]
        kpend = []
        cpend = []
        groups = [GK, GV, GQ, GBG, GCG, GXI] if full else [GK, GV]
        for g in groups:
            ws = g
            wtile = wt[ws]
            if g == GV:
                for ts in range(4):
                    b = 5 + nxt("vb", 2)
                    for kc in range(8):
                        P.op("pe", (lambda b_, ts_, kc_, wtile_: (lambda e: e.matmul(banks[b_][:, :], xn[:, kc_, ts_ * 128:(ts_ + 1) * 128],
                                                                                      wtile_[:, kc_, :], start=(kc_ == 0), stop=(kc_ == 7))))(b, ts, kc, wtile),
                             reads=[("xn", kc), ("wt", ws, 0), ("wt", ws, 1)], writes=[("bank", b)])
                    vs = nxt("vaug", 4)
                    if not stats_done[0]:
                        stats()
                        stats_done[0] = True
                    P.op("dve", (lambda b_, vs_, ts_: (lambda e: e.tensor_scalar(vaug[vs_][:, :, 0:64],
                                                                                  banks[b_][:, :].rearrange("p (h c) -> p h c", c=64),
                                                                                  rstdT[:, ts_:ts_ + 1], None, ALU.mult)))(b, vs, ts),
                         reads=[("bank", b), "rstdT"], writes=[("vaug", vs)])
                    P.op("sp", (lambda vs_, tl_: (lambda e: e.dma_start(out=vscr[:, :, tl_, :].rearrange("h p c -> p h c"), in_=vaug[vs_])))(vs, tile0 + ts),
                         reads=[("vaug", vs)], writes=["vscr"], dma_key="st_v%d" % vs)
                continue
            for c in range(4):
                b = (3, 4, 0, 1)[nxt("fb", 4)]
                for kc in range(8):
                    P.op("pe", (lambda b_, c_, kc_, wtile_: (lambda e: e.matmul(banks[b_][:, :], wtile_[:, kc_, c_ * 128:(c_ + 1) * 128],
                                                                                 xn[:, kc_, :], start=(kc_ == 0), stop=(kc_ == 7))))(b, c, kc, wtile),
                         reads=[("xn", kc), ("wt", ws, 0), ("wt", ws, 1)], writes=[("bank", b)])
                nchunks[0] += 1
                if nchunks[0] == 2 and not stats_done[0]:
                    stats()
                    stats_done[0] = True
                if g == GQ:
                    P.op("dve", (lambda b_, c_: (lambda e: e.tensor_tensor(qst[:, c_, :], banks[b_][:, :], rstd, ALU.mult)))(b, c),
                         reads=[("bank", b), "rstd"], writes=[("qst", c)])
                elif g == GK:
                    def kev(b=b, c=c):
                        P.op("dve", (lambda b_, c_: (lambda e: e.tensor_tensor(kst[:, c_, :], banks[b_][:, :], rstd, ALU.mult)))(b, c),
                             reads=[("bank", b), "rstd"], writes=[("kst", c)])
                    if not stats_done[0]:
                        kpend.append(kev)
                    else:
                        while kpend:
                            kpend.pop(0)()
                        kev()
                elif g == GBG:
                    P.op("dve", (lambda b_, c_: (lambda e: e.tensor_tensor(bgs[:, c_, :], banks[b_][:, :], rstd, ALU.mult)))(b, c),
                         reads=[("bank", b), "rstd"], writes=[("bgs", c)])
                elif g == GCG:
                    P.op("dve", (lambda b_, c_: (lambda e: e.tensor_tensor(cgs[:, c_, :], banks[b_][:, :], rstd2, ALU.mult)))(b, c),
                         reads=[("bank", b), "rstd2"], writes=[("cgs", c)])
                elif g == GXI:
                    u = ubuf[uslot]
                    up = ubuf[1 - uslot]
                    if uprev_scale is None:
                        P.op("pool", (lambda c_, u_, up_: (lambda e: e.tensor_copy(u_[:, c_, 0:2], up_[:, c_, NB:NB + 2])))(c, u, up),
                             reads=[("u", 1 - uslot, c)], writes=[("u", uslot, c)])
                    else:
                        P.op("pool", (lambda c_, u_, up_: (lambda e: e.tensor_scalar(u_[:, c_, 0:2], up_[:, c_, NB:NB + 2],
                                                                                      flags[:, 1:2], None, ALU.mult)))(c, u, up),
                             reads=[("u", 1 - uslot, c), "flags"], writes=[("u", uslot, c)])
                    P.op("dve", (lambda b_, c_, u_: (lambda e: e.tensor_tensor(u_[:, c_, 2:NB + 2], banks[b_][:, :], cgs[:, c_, :], ALU.mult)))(b, c, u),
                         reads=[("bank", b), ("cgs", c), ("u", uslot, c)], writes=[("u", uslot, c)])
                    def convops(c=c, u=u):
                        ct = nxt("ctmp", 2)
                        tt = ctmp[ct]
                        P.op("dve", (lambda c_, u_, tt_: (lambda e: e.tensor_scalar(tt_, u_[:, c_, 2:NB + 2], convw[:, c_, 2:3], None, ALU.mult)))(c, u, tt),
                             reads=[("u", uslot, c), "convw"], writes=[("ctmp", ct)])
                        P.op("dve", (lambda c_, u_, tt_: (lambda e: e.scalar_tensor_tensor(tt_, u_[:, c_, 1:NB + 1], convw[:, c_, 1:2], tt_, ALU.mult, ALU.add)))(c, u, tt),
                             reads=[("u", uslot, c), ("ctmp", ct)], writes=[("ctmp", ct)])
                        P.op("dve", (lambda c_, u_, tt_: (lambda e: e.scalar_tensor_tensor(tt_, u_[:, c_, 0:NB], convw[:, c_, 0:1], tt_, ALU.mult, ALU.add)))(c, u, tt),
                             reads=[("u", uslot, c), ("ctmp", ct)], writes=[("ctmp", ct)])
                        P.op("pool", (lambda c_, tt_: (lambda e: e.tensor_tensor(convst[:, c_, :], tt_, bgs[:, c_, :], ALU.mult)))(c, tt),
                             reads=[("ctmp", ct), ("bgs", c)], writes=[("convst", c)])
                    cpend.append(convops)
            while cpend:
                cpend.pop(0)()
            if g == GQ:
                for half in range(2):
                    P.op("sp", (lambda hf: (lambda e: e.dma_start(
                        out=qscr[:, 0:64, qtok0:qtok0 + NB].rearrange("(c two) d t -> two d c t", two=2)[hf],
                        in_=qst[hf * 64:(hf + 1) * 64, :, :])))(half),
                        reads=[("qst", c) for c in range(4)], writes=["qscr"], dma_key="st_q")
            elif g == GK:
                while deferred:
                    deferred.pop(0)()
                for half in range(2):
                    P.op("sp", (lambda hf: (lambda e: e.dma_start(
                        out=kscr[:, :, tile0 * 128:tile0 * 128 + NB].rearrange("(c two) d t -> two d c t", two=2)[hf],
                        in_=kst[hf * 64:(hf + 1) * 64, :, :])))(half),
                        reads=[("kst", c) for c in range(4)], writes=["kscr"], dma_key="st_k")
            elif g == GXI:
                P.op("sp", lambda e: e.dma_start(out=convscr[:, :, qtok0:qtok0 + NB].rearrange("c p t -> p c t"), in_=convst),
                     reads=[("convst", c) for c in range(4)], writes=["convscr"], dma_key="st_c")
        for ts in range(4):
            for kc in range(8):
                P.op("pe", (lambda ts_, kc_: (lambda e: e.matmul(banks[7][:, ts_ * 8:(ts_ + 1) * 8], xn[:, kc_, ts_ * 128:(ts_ + 1) * 128],
                                                                 wf[:, kc_, :], start=(kc_ == 0), stop=(kc_ == 7))))(ts, kc),
                     reads=[("xn", kc), "wf_sb"], writes=[("bank", 7)])
        lv = lnvs[nxt("lnv", 2)]
        lvk = ("lnv", id(lv))
        for ts in range(4):
            P.op("dve", (lambda ts_: (lambda e: e.scalar_tensor_tensor(zf[:, ts_ * 8:(ts_ + 1) * 8], banks[7][:, ts_ * 8:(ts_ + 1) * 8],
                                                                        rstdT[:, ts_:ts_ + 1], bf32[:, ts_ * 8:(ts_ + 1) * 8], ALU.mult, ALU.add)))(ts),
                 reads=[("bank", 7), "bf32", "rstdT"], writes=["zf"])
        P.op("act", lambda e: e.activation(ef, zf, AF.Exp, scale=-1.0), reads=["zf"], writes=["ef"])
        P.op("act", lambda e: e.activation(lv, ef, AF.Ln, bias=onec, scale=1.0), reads=["ef", "onec"], writes=[lvk])

        pk = nxt("psb", 2)
        psb_ = psb[pk]
        P.op("pool", lambda e: e.tensor_copy(psb_[:, 0, :], prevsum), reads=["prevsum"], writes=[("psb", pk)])
        for ts in range(1, 4):
            P.op("pool", (lambda ts_: (lambda e: e.tensor_tensor(psb_[:, ts_, :], psb_[:, ts_ - 1, :], lv[:, (ts_ - 1) * 8:ts_ * 8], ALU.add)))(ts),
                 reads=[("psb", pk), lvk], writes=[("psb", pk)])
        P.op("pool", lambda e: e.tensor_tensor(prevsum, psb_[:, 3, :], lv[:, 24:32], ALU.add),
             reads=[("psb", pk), lvk], writes=["prevsum"])

        def dchain():
            for ts in range(4):
                tl = tile0 + ts
                reg = banks[7][:, 32 + ts * 8:32 + (ts + 1) * 8]
                P.op("pe", (lambda reg_, ts_: (lambda e: e.matmul(reg_, triu, lv[:, ts_ * 8:(ts_ + 1) * 8], start=True, stop=False)))(reg, ts),
                     reads=["triu", lvk], writes=[("bank", 7)])
                P.op("pe", (lambda reg_, ts_: (lambda e: e.matmul(reg_, onesF, psb_[:, ts_, :], start=False, stop=True)))(reg, ts),
                     reads=["onesF", ("psb", pk)], writes=[("bank", 7)])
                P.op("dve", (lambda reg_, tl_: (lambda e: e.tensor_copy(negD[:, tl_, :], reg_)))(reg, tl),
                     reads=[("bank", 7)], writes=[("negD", tl)])
                if tl < 32:
                    P.op("dve", (lambda reg_, tl_: (lambda e: e.tensor_scalar(negDm[:, tl_, :], reg_, flags[:, 0:1], None, ALU.add)))(reg, tl),
                         reads=[("bank", 7), "flags"], writes=[("negDm", tl)])
            if full:
                for ts in range(4):
                    tl = tile0 + ts
                    P.op("pe", (lambda ts_, tl_: (lambda e: e.transpose(banks[2][0:8, ts_ * 128:(ts_ + 1) * 128], negD[:, tl_, :], identF)))(ts, tl),
                         reads=[("negD", tl), "identF"], writes=[("bank", 2)])
                P.op("dve", lambda e: e.tensor_scalar(drow[0:8, :], banks[2][0:8, :], -8.0, None, ALU.mult),
                     reads=[("bank", 2)], writes=["drow"])
                P.op("sp", lambda e: e.dma_start(out=qscr[:, 64, qtok0:qtok0 + NB], in_=drow[0:8, :]),
                     reads=["drow"], writes=["qscr"], dma_key="st_q")

        deferred.append(dchain)

    srcs = [xctx[cb * NB:(cb + 1) * NB, :] for cb in range(8)] + [xown[ob * NB:(ob + 1) * NB, :] for ob in range(8)]
    blk = 0
    for cb in range(8):
        p1_block(srcs[blk], cb == 7, cb * 4, 0, blk % 2, 0, None, srcs[blk + 1])
        blk += 1
        if later_groups:
            stage_group(later_groups.pop(0))
    for ob in range(8):
        p1_block(srcs[blk], True, 32 + ob * 4, NB + ob * NB, blk % 2, (ob + 1) % 2,
                 True if ob == 0 else None, srcs[blk + 1] if blk + 1 < 16 else None)
        blk += 1
    while deferred:
        deferred.pop(0)()

    P.barrier()
    A.reset()

    KT = [A.alloc(BF16, [TK]) for _ in range(2)]
    VA = [A.alloc(BF16, [64, 128]) for _ in range(2)]
    QT = [A.alloc(BF16, [TQ]) for _ in range(2)]
    PT = [A.alloc(BF16, [NB]) for _ in range(4)]
    rec = [A.alloc(F32, [NB]) for _ in range(2)]
    atts = [A.alloc(BF16, [NB]) for _ in range(2)]
    for s in range(2):
        P.op("pool", (lambda s_: (lambda e: e.memset(KT[s_][64:65, :], 1.0)))(s), writes=[("KT", s)])
    while late_casts:
        cast(*late_casts.pop(0))

    qtiles = [(384, 128, 3968, False)] + [(NB + NB * i, NB, TOWN + NB * i, True) for i in range(8)]
    LOOK = 2

    def issue_head_loads(h):
        hs = h % 2
        P.op("sp", (lambda h_, hs_: (lambda e: e.dma_start(out=KT[hs_][0:64, :], in_=kscr[h_])))(h, hs),
             reads=["kscr"], writes=[("KT", hs)], dma_key="ld_kt%d" % hs)
        P.op("sp", (lambda h_, hs_: (lambda e: e.dma_start(out=VA[hs_], in_=vscr[h_])))(h, hs),
             reads=["vscr"], writes=[("VA", hs)], dma_key="ld_va%d" % hs)
        P.op("sp", (lambda h_, hs_: (lambda e: e.dma_start(out=QT[hs_][0:65, :], in_=qscr[h_])))(h, hs),
             reads=["qscr"], writes=[("QT", hs)], dma_key="ld_qt%d" % hs)

    steps = []
    for h in range(8):
        for qi, (qc0, N, kt0, masked) in enumerate(qtiles):
            nt = (kt0 + N) // 128
            if N == NB:
                order = [0] + list(range(nt - 4, nt)) + list(range(1, nt - 4))
            else:
                order = list(range(nt))
            for pos, j in enumerate(order):
                steps.append((h, qi, j, nt, pos == 0, pos == nt - 1))
    info = {}

    def emit_front(idx):
        h, qi, j, nt, first, last = steps[idx]
        hs = h % 2
        qc0, N, kt0, masked = qtiles[qi]
        if first:
            ab = 4 + nxt("accb", 3)
            info[(h, qi)] = ab
        s0 = j * 128
        c0 = max(0, s0 - kt0)
        diag = s0 >= kt0
        sb = nxt("sbank", 4)
        S = banks[sb]
        if not diag:
            P.op("pe", (lambda S_, hs_, s0_, c0_, qc0_, N_: (lambda e: e.matmul(
                S_[:, c0_:N_], KT[hs_][0:65, s0_:s0_ + 128], QT[hs_][0:65, qc0_ + c0_:qc0_ + N_],
                start=True, stop=True)))(S, hs, s0, c0, qc0, N),
                reads=[("KT", hs), ("QT", hs)], writes=[("bank", sb)])
        else:
            P.op("pe", (lambda S_, hs_, s0_, c0_, qc0_: (lambda e: e.matmul(
                S_[:, c0_:c0_ + 128], KT[hs_][0:65, s0_:s0_ + 128], QT[hs_][0:65, qc0_ + c0_:qc0_ + c0_ + 128],
                start=True, stop=False)))(S, hs, s0, c0, qc0),
                reads=[("KT", hs), ("QT", hs)], writes=[("bank", sb)])
            P.op("pe", (lambda S_, c0_: (lambda e: e.matmul(S_[:, c0_:c0_ + 128], identB, maskT, start=False, stop=True)))(S, c0),
                 reads=["identB", "maskT"], writes=[("bank", sb)])
            if c0 + 128 < N:
                P.op("pe", (lambda S_, hs_, s0_, c0_, qc0_, N_: (lambda e: e.matmul(
                    S_[:, c0_ + 128:N_], KT[hs_][0:65, s0_:s0_ + 128], QT[hs_][0:65, qc0_ + c0_ + 128:qc0_ + N_],
                    start=True, stop=True)))(S, hs, s0, c0, qc0, N),
                    reads=[("KT", hs), ("QT", hs)], writes=[("bank", sb)])
        bias = (negDm if (masked and j < 32) else negD)[:, j, h:h + 1]
        ps = nxt("pt", 4)
        P.op("act", (lambda S_, ps_, c0_, N_, bias_: (lambda e: e.activation(PT[ps_][:, c0_:N_], S_[:, c0_:N_], AF.Exp,
                                                                               bias=bias_, scale=0.125)))(S, ps, c0, N, bias),
             reads=[("bank", sb)], writes=[("PT", ps)])
        info[idx] = (ps, c0)

    def emit_back(idx):
        h, qi, j, nt, first, last = steps[idx]
        hs = h % 2
        qc0, N, kt0, masked = qtiles[qi]
        ps, c0 = info.pop(idx)
        ab = info[(h, qi)]
        acc = banks[ab]
        P.op("pe", (lambda acc_, hs_, j_, ps_, c0_, N_, first_, last_: (lambda e: e.matmul(
            acc_[:, c0_:N_], VA[hs_][:, j_, :], PT[ps_][:, c0_:N_], start=first_, stop=last_,
            skip_group_check=True)))(acc, hs, j, ps, c0, N, first, last),
            reads=[("VA", hs), ("PT", ps)], writes=[("bank", ab)])
        if last:
            rs = nxt("rec", 2)
            P.op("dve", (lambda acc_, rs_, N_: (lambda e: e.reciprocal(rec[rs_][0:64, 0:N_], acc_[64:128, 0:N_])))(acc, rs, N),
                 reads=[("bank", ab)], writes=[("rec", rs)])
            P.op("dve", (lambda acc_, rs_, N_: (lambda e: e.tensor_tensor(atts[rs_][0:64, 0:N_], acc_[0:64, 0:N_], rec[rs_][0:64, 0:N_], ALU.mult)))(acc, rs, N),
                 reads=[("bank", ab), ("rec", rs)], writes=[("atts", rs)])
            P.op("sp", (lambda h_, rs_, qc0_, N_: (lambda e: e.dma_start(out=attscr[h_, :, qc0_:qc0_ + N_], in_=atts[rs_][0:64, 0:N_])))(h, rs, qc0, N),
                 reads=[("atts", rs)], writes=["attscr"], dma_key="st_a%d" % rs)

    issue_head_loads(0)
    issue_head_loads(1)
    for idx in range(len(steps) + LOOK):
        if idx < len(steps):
            emit_front(idx)
        if idx >= LOOK:
            bi = idx - LOOK
            emit_back(bi)
            hprev = steps[bi][0]
            if (bi + 1 == len(steps) or steps[bi + 1][0] != hprev) and hprev + 2 < 8:
                issue_head_loads(hprev + 2)

    P.barrier()
    A.reset()

    xtk = [A.alloc(F32, [4, D]) for _ in range(2)]
    h3 = A.alloc(F32, [8, NB])
    sq3 = A.alloc(BF16, [8, NB])
    xn3 = A.alloc(BF16, [8, NB])
    rstd3 = A.alloc(F32, [NB])
    t13 = A.alloc(F32, [NB])
    hid = A.alloc(BF16, [32, NB])
    R1 = hid
    wo = R1[:, 0:16, :].rearrange("p a b -> p (a b)").rearrange("p (k c) -> p k c", c=D)
    attT = R1[:, 16:20, :]
    convT = A.alloc(BF16, [4, NB])
    wup = [A.alloc(BF16, [8, 512]) for _ in range(3)]
    wdn = [A.alloc(BF16, [32, 128]) for _ in range(2)]
    poolw = A.alloc(BF16, [4, 2, 256])
    zz = A.alloc(F32, [8, NB + 16])
    pa = A.alloc(F32, [NB + 16])
    pb = A.alloc(F32, [NB + 16])
    pc = A.alloc(F32, [NB + 16])
    pd = A.alloc(F32, [NB + 16])
    pooled = A.alloc(BF16, [8, NB])
    rr = [A.alloc(F32, [NB]) for _ in range(3)]
    outst = [A.alloc(F32, [D]) for _ in range(2)]

    for g in range(4):
        P.op("sp", (lambda g_: (lambda e: e.dma_start(out=poolw[:, g_, :, :], in_=poolw_s[g_])))(g),
             reads=["poolw"], writes=["poolw_sb"], dma_key="ld_pw")

    def ffn(l, N, gi):
        for kc in range(8):
            if kc % 2 == 0:
                P.op("act", (lambda kc_: (lambda e: e.activation(xn3[:, kc_, 0:N], h3[:, kc_, 0:N], AF.Copy,
                                                                 scale=vecs[:, gi, kc_:kc_ + 1])))(kc),
                     reads=[("h3", kc), "vecs"], writes=[("xn3", kc)])
            else:
                P.op("dve", (lambda kc_: (lambda e: e.tensor_scalar(xn3[:, kc_, 0:N], h3[:, kc_, 0:N], vecs[:, gi, kc_:kc_ + 1],
                                                                    None, ALU.mult)))(kc),
                     reads=[("h3", kc), "vecs"], writes=[("xn3", kc)])
        for kc in range(8):
            if kc % 2 == 1:
                P.op("act", (lambda kc_: (lambda e: e.activation(sq3[:, kc_, 0:N], h3[:, kc_, 0:N], AF.Square)))(kc),
                     reads=[("h3", kc)], writes=[("sq", kc)])
            else:
                P.op("pool", (lambda kc_: (lambda e: e.tensor_tensor(sq3[:, kc_, 0:N], h3[:, kc_, 0:N], h3[:, kc_, 0:N], ALU.mult)))(kc),
                     reads=[("h3", kc)], writes=[("sq", kc)])

        def stats():
            for kc in range(8):
                P.op("pe", (lambda kc_: (lambda e: e.matmul(banks[2][:, 0:N], onesB, sq3[:, kc_, 0:N],
                                                           start=(kc_ == 0), stop=(kc_ == 7))))(kc),
                     reads=[("sq", kc), "onesB"], writes=[("bank", 2)])
            P.op("act", lambda e: e.activation(t13[:, 0:N], banks[2][:, 0:N], AF.Ln, bias=epsc, scale=1.0 / D),
                 reads=[("bank", 2), "epsc"], writes=["t1"])
            P.op("act", lambda e: e.activation(rstd3[:, 0:N], t13[:, 0:N], AF.Exp, scale=-0.5),
                 reads=["t1"], writes=["rstd"])

        pending = []

        def evac(item):
            b, fc = item
            rs = nxt("rr", 3)
            P.op("dve", (lambda b_, rs_: (lambda e: e.scalar_tensor_tensor(rr[rs_][:, 0:N], banks[b_][:, 0:N], 0.0, rstd3[:, 0:N],
                                                                            ALU.max, ALU.mult)))(b, rs),
                 reads=[("bank", b), "rstd"], writes=[("rr", rs)])
            P.op("pool", (lambda fc_, rs_: (lambda e: e.tensor_tensor(hid[:, fc_, 0:N], rr[rs_][:, 0:N], rr[rs_][:, 0:N], ALU.mult)))(fc, rs),
                 reads=[("rr", rs)], writes=[("hid", fc)])

        nchunk = 0
        for g in range(8):
            ws = nxt("wup", 3)
            P.op("sp", (lambda g_, ws_: (lambda e: e.dma_start(out=wup[ws_], in_=wup_s[l][g_])))(g, ws),
                 reads=[("wup", l)], writes=[("wup_sb", ws)], dma_key="ld_wu%d" % ws)
            for c in range(4):
                fc = g * 4 + c
                b = 3 + nxt("fb3", 3)
                for kc in range(8):
                    P.op("pe", (lambda b_, c_, kc_, ws_: (lambda e: e.matmul(banks[b_][:, 0:N], wup[ws_][:, kc_, c_ * 128:(c_ + 1) * 128],
                                                                              xn3[:, kc_, 0:N], start=(kc_ == 0), stop=(kc_ == 7))))(b, c, kc, ws),
                         reads=[("xn3", kc), ("wup_sb", ws)], writes=[("bank", b)])
                nchunk += 1
                pending.append((b, fc))
                if nchunk == 2:
                    stats()
                if nchunk >= 2:
                    while pending:
                        evac(pending.pop(0))
        for oc in range(8):
            ws = nxt("wdn", 2)
            P.op("sp", (lambda oc_, ws_: (lambda e: e.dma_start(out=wdn[ws_], in_=wdn_s[l][oc_])))(oc, ws),
                 reads=[("wdn", l)], writes=[("wdn_sb", ws)], dma_key="ld_wd%d" % ws)
            b = 3 + nxt("fb3", 3)
            for kc in range(32):
                P.op("pe", (lambda b_, kc_, ws_: (lambda e: e.matmul(banks[b_][:, 0:N], wdn[ws_][:, kc_, :], hid[:, kc_, 0:N],
                                                                      start=(kc_ == 0), stop=(kc_ == 31))))(b, kc, ws),
                     reads=[("hid", kc), ("wdn_sb", ws)], writes=[("bank", b)])
            P.op("dve", (lambda b_, oc_: (lambda e: e.tensor_tensor(h3[:, oc_, 0:N], banks[b_][:, 0:N], h3[:, oc_, 0:N], ALU.add)))(b, oc),
                 reads=[("bank", b), ("h3", oc)], writes=[("h3", oc)])

    HID_ALL = [("hid", fc) for fc in range(32)]

    def issue_opnd_loads(qtok0, N):
        P.op("sp", lambda e: e.dma_start(out=wo, in_=wo_s), reads=["wo"], writes=HID_ALL, dma_key="ld_wo")
        for hf in range(2):
            P.op("sp", (lambda hf_: (lambda e: e.dma_start(
                out=attT[hf_ * 64:(hf_ + 1) * 64, :, 0:N],
                in_=attscr[:, :, qtok0:qtok0 + N].rearrange("(c two) d t -> two d c t", two=2)[hf_])))(hf),
                reads=["attscr"], writes=HID_ALL, dma_key="ld_wo")
        P.op("sp", lambda e: e.dma_start(out=convT[:, :, 0:N], in_=convscr[:, :, qtok0:qtok0 + N].rearrange("c p t -> p c t")),
             reads=["convscr"], writes=["convT"], dma_key="ld_wo")

    def p3_block(xs, nts, qtok0, out_rows, first_own, nxt_src, nxt_nts, nxt_q):
        N = nts * 128
        transpose_x(nts, xtk[xs], ("xtk", xs), h3, "h3")
        if nxt_src is not None:
            issue_x_load(nxt_src, nxt_nts, xtk[1 - xs], ("xtk", 1 - xs), "ld_x3%d" % (1 - xs))
        for oc in range(8):
            b = 3 + nxt("fb3", 3)
            for k in range(8):
                rhs_ = attT[:, k, 0:N] if k < 4 else convT[:, k - 4, 0:N]
                P.op("pe", (lambda b_, oc_, k_, rhs__: (lambda e: e.matmul(banks[b_][:, 0:N], wo[:, k_, oc_ * 128:(oc_ + 1) * 128],
                                                                            rhs__, start=(k_ == 0), stop=(k_ == 7))))(b, oc, k, rhs_),
                     reads=[HID_ALL[0], "convT"], writes=[("bank", b)])
            P.op("dve", (lambda b_, oc_: (lambda e: e.tensor_tensor(h3[:, oc_, 0:N], banks[b_][:, 0:N], h3[:, oc_, 0:N], ALU.add)))(b, oc),
                 reads=[("bank", b), ("h3", oc)], writes=[("h3", oc)])
        P.op("pe", lambda e: e.matmul(banks[2][0:8, 0:8], onesB[:, 0:8], onesB[:, 0:8], start=True, stop=True),
             reads=HID_ALL + ["onesB"], writes=[("bank", 2)])
        ffn(0, N, 1)
        if qtok0 < NB and nxt_q is not None:
            issue_opnd_loads(nxt_q, nxt_nts * 128)
        rmsnorm(h3, "h3", N, 2, sq3, zz[:, :, 16:16 + NB], rstd3, t13, xnkey="zz")
        if qtok0 < NB:
            for kc in range(8):
                P.op("dve", (lambda kc_: (lambda e: e.tensor_scalar(zhalo[:, kc_, 1:16], zz[:, kc_, 16 + N - 15:16 + N], flags[:, 1:2], None, ALU.mult)))(kc),
                     reads=[("zz", kc), "flags"], writes=[("zhalo", kc)])
            return
        for kc in range(8):
            g = kc // 2
            w = 2 << g
            eng = "pool" if kc in (0, 1, 2, 4) else "dve"
            P.op("act", (lambda kc_: (lambda e: e.activation(zz[:, kc_, 1:16], zhalo[:, kc_, 1:16], AF.Copy)))(kc),
                 reads=[("zhalo", kc)], writes=[("zzh", kc)])
            src = zz[:, kc, :]
            lo = 1
            sh = 1
            bufs = [pa, pb] if eng == "pool" else [pc, pd]
            bi = 0
            srckeys = [("zz", kc), ("zzh", kc)]
            while sh < w:
                dst = bufs[bi]
                lo2 = lo + sh
                P.op(eng, (lambda dst_, src_, lo2_, sh_: (lambda e: e.tensor_tensor(dst_[:, lo2_:16 + N], src_[:, lo2_:16 + N],
                                                                                     src_[:, lo2_ - sh_:16 + N - sh_], ALU.add)))(dst, src, lo2, sh),
                     reads=srckeys, writes=[("pbuf", eng, bi)])
                src = dst
                srckeys = [("pbuf", eng, bi)]
                lo = lo2
                sh *= 2
                bi = 1 - bi
            P.op("dve", (lambda kc_, src_, w_: (lambda e: e.scalar_tensor_tensor(pooled[:, kc_, 0:N], src_[:, 16:16 + N], 1.0 / w_,
                                                                                  zz[:, kc_, 16:16 + N], ALU.mult, ALU.subtract)))(kc, src, w),
                 reads=srckeys + [("zz", kc)], writes=[("pooled", kc)])
            if first_own:
                other = bufs[bi]
                P.op("dve", (lambda src_, other_, g_: (lambda e: e.tensor_tensor(other_[:, 0:16], src_[:, 16:32], invcnt[:, g_, :], ALU.mult)))(src, other, g),
                     reads=srckeys + ["invcnt"], writes=[("pbuf", eng, bi)])
                P.op("dve", (lambda kc_, other_: (lambda e: e.tensor_tensor(pooled[:, kc_, 0:16], other_[:, 0:16], zz[:, kc_, 16:32], ALU.subtract)))(kc, other),
                     reads=[("pbuf", eng, bi), ("zz", kc)], writes=[("pooled", kc)])
        for kc in range(8):
            P.op("pool", (lambda kc_: (lambda e: e.tensor_copy(zhalo[:, kc_, 1:16], zz[:, kc_, 16 + N - 15:16 + N])))(kc),
                 reads=[("zz", kc), ("zzh", kc)], writes=[("zhalo", kc)])
        for g in range(4):
            for dc in range(2):
                oc = 2 * g + dc
                b = 3 + nxt("fb3", 3)
                for cc in range(2):
                    P.op("pe", (lambda b_, g_, cc_, dc_: (lambda e: e.matmul(banks[b_][:, 0:N], poolw[:, g_, cc_, dc_ * 128:(dc_ + 1) * 128],
                                                                              pooled[:, 2 * g_ + cc_, 0:N], start=(cc_ == 0), stop=(cc_ == 1))))(b, g, cc, dc),
                         reads=[("pooled", 2 * g + cc), "poolw_sb"], writes=[("bank", b)])
                P.op("dve", (lambda b_, oc_: (lambda e: e.scalar_tensor_tensor(h3[:, oc_, 0:N], banks[b_][:, 0:N], vecs[:, 5, oc_:oc_ + 1],
                                                                                h3[:, oc_, 0:N], ALU.mult, ALU.add)))(b, oc),
                     reads=[("bank", b), ("h3", oc), "vecs"], writes=[("h3", oc)])
        ffn(1, N, 3)
        if nxt_q is not None:
            issue_opnd_loads(nxt_q, nxt_nts * 128)
        rmsnorm(h3, "h3", N, 4, sq3, zz[:, :, 16:16 + NB], rstd3, t13, xnkey="zz")
        for ts in range(nts):
            os_ = nxt("outst", 2)
            for half in range(2):
                b = (6, 7)[half] if (ts % 2 == 0) else (0, 1)[half]
                for k4 in range(4):
                    kc = half * 4 + k4
                    P.op("pe", (lambda b_, k4_, kc_, ts_: (lambda e: e.transpose(banks[b_][:, k4_ * 128:(k4_ + 1) * 128],
                                                                                  zz[:, kc_, 16 + ts_ * 128:16 + (ts_ + 1) * 128], identF)))(b, k4, kc, ts),
                         reads=[("zz", kc), "identF"], writes=[("bank", b)])
                P.op("act", (lambda b_, half_, os__: (lambda e: e.activation(outst[os__][:, half_ * 512:(half_ + 1) * 512], banks[b_][:, :], AF.Copy)))(b, half, os_),
                     reads=[("bank", b)], writes=[("outst", os_, half)])
            P.op("act", (lambda ts_, os__: (lambda e: e.dma_start(out=out_rows[ts_ * 128:(ts_ + 1) * 128, :], in_=outst[os__])))(ts, os_),
                 reads=[("outst", os_, 0), ("outst", os_, 1)], dma_key="st_o%d" % os_)

    issue_x_load(xctx[TOWN - 128:TOWN, :], 1, xtk[0], ("xtk", 0), "ld_x30")
    issue_opnd_loads(384, 128)
    p3_block(0, 1, 384, None, False, xown[0:NB, :], 4, NB)
    for ob in range(8):
        p3_block((ob + 1) % 2, 4, NB + ob * NB, out_d[ob * NB:(ob + 1) * NB, :], ob == 0,
                 xown[(ob + 1) * NB:(ob + 2) * NB, :] if ob < 7 else None, 4, NB + (ob + 1) * NB if ob < 7 else None)

    P.finalize()
    names = P.sem_names()
    sems = {k: E(nc.semaphore("s%d" % i)) for i, k in enumerate(names)}
    block = E(nc.Block())

    @block.tensor
    def _(e):
        P.run_engine("pe", e, sems)

    @block.scalar
    def _(e):
        P.run_engine("act", e, sems)

    @block.vector
    def _(e):
        P.run_engine("dve", e, sems)

    @block.gpsimd
    def _(e):
        P.run_engine("pool", e, sems)

    @block.sync
    def _(e):
        P.run_engine("sp", e, sems)
        P.final_waits(e, sems)

    st.close()
    return nc


_NC = None


def kernel(x, norm_mix_0, w_in_0, b_f_0, conv_w_0, w_out_0, norm_ffn_0, w_up_0, w_down_0,
           norm_mix_1, pool_w_1, pool_scale_1, norm_ffn_1, w_up_1, w_down_1, final_norm):
    global _NC
    f = lambda a: np.ascontiguousarray(np.asarray(a, dtype=np.float32))
    x = f(x)
    if _NC is None:
        _NC = build_nc()
    nc = _NC

    def pk(v):
        return f(v).reshape(8, 128).T

    vecs = np.ascontiguousarray(np.stack([pk(norm_mix_0), pk(norm_ffn_0), pk(norm_mix_1), pk(norm_ffn_1),
                                          pk(final_norm), pk(pool_scale_1)], axis=1))
    bf32 = np.ascontiguousarray(np.broadcast_to(np.tile(f(b_f_0), 4)[None, :], (128, 32)))
    convw = np.ascontiguousarray(f(conv_w_0).reshape(3, 4, 128).transpose(2, 1, 0))
    shared = {
        "vecs": vecs, "bf32": bf32, "convw": convw,
        "w_in": f(w_in_0), "w_out": f(w_out_0), "w_up0": f(w_up_0), "w_up1": f(w_up_1),
        "w_dn0": f(w_down_0), "w_dn1": f(w_down_1), "pool_w": f(pool_w_1),
    }
    in_maps = []
    for c in range(NCORES):
        b, half = c // 2, c % 2
        flags = np.zeros((128, 4), np.float32)
        flags[:, 0] = 0.0 if half == 1 else -30000.0
        flags[:, 1] = 1.0 if half == 1 else 0.0
        inv = np.zeros((4, 16), np.float32)
        for g in range(4):
            w = 2 << g
            for j in range(16):
                inv[g, j] = 1.0 / w if half == 1 else 1.0 / min(j + 1, w)
        invcnt = np.ascontiguousarray(np.broadcast_to(inv.reshape(1, 64), (128, 64)))
        m = dict(shared)
        m["xown"] = np.ascontiguousarray(x[b, half * TOWN:(half + 1) * TOWN])
        m["xctx"] = np.ascontiguousarray(x[b, 0:TOWN])
        m["flags"] = flags
        m["invcnt"] = invcnt
        in_maps.append(m)
    res = run_bass_kernel_spmd(nc, in_maps, core_ids=list(range(NCORES)))
    out = np.empty((4, 8192, D), np.float32)
    for c in range(NCORES):
        b, half = c // 2, c % 2
        out[b, half * TOWN:(half + 1) * TOWN] = res.results[c]["out"]
    return out
```
